# Optimizing a Trainium2 kernel written in Bass

```python
import math
import jax, jax.numpy as jnp
from jax import lax
import numpy as np

D_MODEL = 4096
BATCH = 2
SEQ = 8192
DEPTH = 2
DEC_BATCH = 4
DEC_SEQ = 4096
PAST_LEN = 128

PLE_DIM = 256
GRID_W = 64
MIX_WIDTH = D_MODEL
POOL_WIDTH = MIX_WIDTH // 4
POOL_WINDOWS = (2, 4, 8, 16)
POOL_GROUP = POOL_WIDTH // len(POOL_WINDOWS)
MLSTM_WIDTH = MIX_WIDTH // 4
MLSTM_HEADS = 4
MLSTM_HEAD_DIM = MLSTM_WIDTH // MLSTM_HEADS
MLSTM_CHUNK = 64
ATTN_WIDTH = MIX_WIDTH - POOL_WIDTH - MLSTM_WIDTH
ATTN_HEAD_DIM = 128
ATTN_HEADS = ATTN_WIDTH // ATTN_HEAD_DIM
ATTN_KV_HEADS = 4
ATTN_GROUP = ATTN_HEADS // ATTN_KV_HEADS
KV_WIDTH = ATTN_KV_HEADS * ATTN_HEAD_DIM
ROPE_AXIS_DIM = ATTN_HEAD_DIM // 2
ROPE_THETA = 10000.0
Q_BLOCK = 128
N_GATES = 4 * MLSTM_HEADS
EPS = 1e-6
IN_SPLITS = (POOL_WIDTH, POOL_WIDTH,
             MLSTM_WIDTH, MLSTM_WIDTH, MLSTM_WIDTH, MLSTM_WIDTH, MLSTM_WIDTH, N_GATES,
             ATTN_WIDTH, KV_WIDTH, KV_WIDTH, ATTN_WIDTH)
IN_WIDTH = sum(IN_SPLITS)

kernel_name = 'hybrid_pool_mlstm_gqa_encoder'


def rmsnorm(x, g):
    xf = x.astype(jnp.float32)
    r = lax.rsqrt(jnp.mean(xf * xf, axis=-1, keepdims=True) + EPS)
    return (xf * r * g.astype(jnp.float32)).astype(x.dtype)


def pool_mixer(u, z, w_pool, scale):
    S = u.shape[1]
    uf = u.astype(jnp.float32)
    cs = jnp.concatenate([jnp.zeros_like(uf[:, :1]), jnp.cumsum(uf, axis=1)], axis=1)
    t = jnp.arange(S)
    outs = []
    for g, w in enumerate(POOL_WINDOWS):
        sl = slice(g * POOL_GROUP, (g + 1) * POOL_GROUP)
        lo = jnp.clip(t - w // 2, 0, S - 1)
        hi = jnp.clip(t + w // 2 - 1, 0, S - 1)
        cnt = (hi - lo + 1).astype(jnp.float32)
        csg = cs[..., sl]
        mean = (jnp.take(csg, hi + 1, axis=1) - jnp.take(csg, lo, axis=1)) / cnt[None, :, None]
        d = (mean - uf[..., sl]).astype(u.dtype)
        outs.append(jnp.einsum('bsc,cd->bsd', d, w_pool[g]))
    y = jnp.concatenate(outs, axis=-1) * scale
    return y * jax.nn.silu(z)


def mlstm_scan(q, k, v, logi, logf):
    B, H, S, dk = q.shape
    dv = v.shape[-1]
    L = MLSTM_CHUNK
    NC = S // L

    def chunks(a):
        return jnp.moveaxis(a.reshape(B, H, NC, L, *a.shape[3:]), 2, 0)

    mask = jnp.tril(jnp.ones((L, L), dtype=bool))

    def body(carry, inp):
        C, n, m = carry
        qc, kc, vc, li, lf = inp
        b = jnp.cumsum(lf, axis=-1)
        D = b[..., :, None] - b[..., None, :] + li[..., None, :]
        D = jnp.where(mask, D, -jnp.inf)
        m_inter = b + m[..., None]
        m_row = jnp.maximum(m_inter, jnp.max(D, axis=-1))
        w_intra = jnp.exp(D - m_row[..., None])
        w_inter = jnp.exp(m_inter - m_row)
        s = jnp.einsum('bhid,bhjd->bhij', qc, kc) * w_intra
        num = (jnp.einsum('bhij,bhje->bhie', s, vc)
               + w_inter[..., None] * jnp.einsum('bhid,bhde->bhie', qc, C))
        den = jnp.sum(s, axis=-1) + w_inter * jnp.einsum('bhid,bhd->bhi', qc, n)
        h = num / jnp.maximum(jnp.abs(den), jnp.exp(-m_row))[..., None]
        bL = b[..., -1]
        w_r = bL[..., None] - b + li
        m_new = jnp.maximum(bL + m, jnp.max(w_r, axis=-1))
        decay = jnp.exp(bL + m - m_new)
        wk = jnp.exp(w_r - m_new[..., None])
        C_new = decay[..., None, None] * C + jnp.einsum('bhj,bhjd,bhje->bhde', wk, kc, vc)
        n_new = decay[..., None] * n + jnp.einsum('bhj,bhjd->bhd', wk, kc)
        return (C_new, n_new, m_new), h

    init = (jnp.zeros((B, H, dk, dv), jnp.float32),
            jnp.zeros((B, H, dk), jnp.float32),
            jnp.zeros((B, H), jnp.float32))
    _, hs = lax.scan(body, init, (chunks(q), chunks(k), chunks(v), chunks(logi), chunks(logf)))
    return jnp.moveaxis(hs, 0, 2).reshape(B, H, S, dv)


def mlstm_mixer(q, k, v, o, z, gates, b_gate, norm_g):
    B, S, _ = q.shape

    def heads(a):
        return a.astype(jnp.float32).reshape(B, S, MLSTM_HEADS, MLSTM_HEAD_DIM).transpose(0, 2, 1, 3)

    qh = heads(q) * (MLSTM_HEAD_DIM ** -0.5)
    kh = heads(k)
    vh = heads(v)
    gt = (gates.astype(jnp.float32) + b_gate.astype(jnp.float32)).transpose(0, 2, 1)
    i_f, f_f, i_b, f_b = jnp.split(gt, 4, axis=1)
    h_fwd = mlstm_scan(qh, kh, vh, i_f, jax.nn.log_sigmoid(f_f))
    flip = lambda a: jnp.flip(a, axis=2)
    h_bwd = flip(mlstm_scan(flip(qh), flip(kh), flip(vh), flip(i_b), flip(jax.nn.log_sigmoid(f_b))))
    h = (h_fwd + h_bwd).transpose(0, 2, 1, 3)
    h = h * lax.rsqrt(jnp.mean(h * h, axis=-1, keepdims=True) + EPS)
    h = h.reshape(B, S, MLSTM_WIDTH) * norm_g.astype(jnp.float32)
    h = h * jax.nn.sigmoid(o.astype(jnp.float32))
    return h.astype(q.dtype) * jax.nn.silu(z)


def rope_half(x, ang):
    c = jnp.cos(ang)[None, :, None, :]
    s = jnp.sin(ang)[None, :, None, :]
    xa, xb = jnp.split(x, 2, axis=-1)
    return jnp.concatenate([xa * c - xb * s, xb * c + xa * s], axis=-1)


def rope_2d(x, ang_r, ang_c):
    xr, xc = jnp.split(x, 2, axis=-1)
    return jnp.concatenate([rope_half(xr, ang_r), rope_half(xc, ang_c)], axis=-1)


def attn_mixer(q, k, v, z, q_norm, k_norm, rows):
    B, S, _ = q.shape
    qh = rmsnorm(q.reshape(B, S, ATTN_HEADS, ATTN_HEAD_DIM), q_norm).astype(jnp.float32)
    kh = rmsnorm(k.reshape(B, S, ATTN_KV_HEADS, ATTN_HEAD_DIM), k_norm).astype(jnp.float32)
    vh = v.reshape(B, S, ATTN_KV_HEADS, ATTN_HEAD_DIM).astype(jnp.float32)
    row_ids = jnp.repeat(jnp.arange(rows), GRID_W).astype(jnp.float32)
    col_ids = jnp.tile(jnp.arange(GRID_W), rows).astype(jnp.float32)
    nf = ROPE_AXIS_DIM // 2
    inv = ROPE_THETA ** (-jnp.arange(nf, dtype=jnp.float32) / nf)
    ang_r = row_ids[:, None] * inv
    ang_c = col_ids[:, None] * inv
    qh = rope_2d(qh, ang_r, ang_c)
    kh = rope_2d(kh, ang_r, ang_c)
    NB = S // Q_BLOCK
    qb = qh.reshape(B, NB, Q_BLOCK, ATTN_KV_HEADS, ATTN_GROUP, ATTN_HEAD_DIM).transpose(1, 0, 3, 4, 2, 5)
    kt = kh.transpose(0, 2, 1, 3)
    vt = vh.transpose(0, 2, 1, 3)
    scale = ATTN_HEAD_DIM ** -0.5

    def block(qblk):
        s = jnp.einsum('bkgqd,bksd->bkgqs', qblk, kt) * scale
        p = jax.nn.softmax(s, axis=-1)
        return jnp.einsum('bkgqs,bksd->bkgqd', p, vt)

    ob = lax.map(block, qb)
    o = ob.transpose(1, 0, 4, 2, 3, 5).reshape(B, S, ATTN_WIDTH)
    return o.astype(q.dtype) * jax.nn.silu(z)


def layer(x, p, rows, norm_pre, w_in, b_gate, w_pool, pool_scale, mlstm_norm,
          q_norm, k_norm, w_out, norm_post, w_ple_proj, w_ple_gate, ple_norm):
    h = rmsnorm(x, norm_pre)
    zin = jnp.einsum('bsd,de->bse', h, w_in)
    offs = np.cumsum(IN_SPLITS)[:-1].tolist()
    (pu, pz, mq, mk, mv, mo, mz, mg, aq, ak, av, az) = jnp.split(zin, offs, axis=-1)
    y_pool = pool_mixer(pu, pz, w_pool, pool_scale)
    y_ml = mlstm_mixer(mq, mk, mv, mo, mz, mg, b_gate, mlstm_norm)
    y_att = attn_mixer(aq, ak, av, az, q_norm, k_norm, rows)
    y = jnp.concatenate([y_pool, y_ml, y_att], axis=-1)
    y = jnp.einsum('bsc,cd->bsd', y, w_out)
    x = x + rmsnorm(y, norm_post)
    ple = jnp.einsum('bsr,rd->bsd', p, w_ple_proj) * jax.nn.sigmoid(jnp.einsum('bsd,de->bse', x, w_ple_gate))
    return x + rmsnorm(ple, ple_norm)


def trunk(x, p, norm_pre, w_in, b_gate, w_pool, pool_scale, mlstm_norm,
          q_norm, k_norm, w_out, norm_post, w_ple_proj, w_ple_gate, ple_norm):
    rows = x.shape[1] // GRID_W
    for i in range(DEPTH):
        x = layer(x, p[i], rows, norm_pre[i], w_in[i], b_gate[i], w_pool[i], pool_scale[i],
                  mlstm_norm[i], q_norm[i], k_norm[i], w_out[i], norm_post[i],
                  w_ple_proj[i], w_ple_gate[i], ple_norm[i])
    return x


def setup_inputs(seed: int = 0) -> dict:
    key = jax.random.key(seed)
    ks = jax.random.split(key, 20)
    f32 = jnp.float32
    nrm = lambda k, shape: jax.random.normal(k, shape, f32)
    x_prompt = nrm(ks[0], (BATCH, SEQ, D_MODEL))
    x_sample = nrm(ks[1], (DEC_BATCH, DEC_SEQ, D_MODEL))
    p_prompt = nrm(ks[2], (DEPTH, BATCH, SEQ, PLE_DIM))
    p_sample = nrm(ks[3], (DEPTH, DEC_BATCH, DEC_SEQ, PLE_DIM))
    norm_pre = 1.0 + 0.1 * nrm(ks[4], (DEPTH, D_MODEL))
    w_in = nrm(ks[5], (DEPTH, D_MODEL, IN_WIDTH)) * (D_MODEL ** -0.5)
    i_base = jnp.full((MLSTM_HEADS,), -1.0, f32)
    f_base = jnp.linspace(3.0, 6.0, MLSTM_HEADS, dtype=f32)
    gate_base = jnp.concatenate([i_base, f_base, i_base, f_base])
    b_gate = gate_base[None, :] + 0.1 * nrm(ks[6], (DEPTH, N_GATES))
    w_pool = nrm(ks[7], (DEPTH, len(POOL_WINDOWS), POOL_GROUP, POOL_GROUP)) * (POOL_GROUP ** -0.5)
    pool_scale = 1.0 + 0.1 * nrm(ks[8], (DEPTH, POOL_WIDTH))
    mlstm_norm = 1.0 + 0.1 * nrm(ks[9], (DEPTH, MLSTM_WIDTH))
    q_norm = 1.0 + 0.1 * nrm(ks[10], (DEPTH, ATTN_HEAD_DIM))
    k_norm = 1.0 + 0.1 * nrm(ks[11], (DEPTH, ATTN_HEAD_DIM))
    w_out = nrm(ks[12], (DEPTH, MIX_WIDTH, D_MODEL)) * (MIX_WIDTH ** -0.5)
    norm_post = 1.0 + 0.1 * nrm(ks[13], (DEPTH, D_MODEL))
    w_ple_proj = nrm(ks[14], (DEPTH, PLE_DIM, D_MODEL)) * (PLE_DIM ** -0.5)
    w_ple_gate = nrm(ks[15], (DEPTH, D_MODEL, D_MODEL)) * (D_MODEL ** -0.5)
    ple_norm = 1.0 + 0.1 * nrm(ks[16], (DEPTH, D_MODEL))
    return {'x_prompt': x_prompt, 'x_sample': x_sample, 'p_prompt': p_prompt, 'p_sample': p_sample,
            'norm_pre': norm_pre, 'w_in': w_in, 'b_gate': b_gate, 'w_pool': w_pool,
            'pool_scale': pool_scale, 'mlstm_norm': mlstm_norm, 'q_norm': q_norm, 'k_norm': k_norm,
            'w_out': w_out, 'norm_post': norm_post, 'w_ple_proj': w_ple_proj,
            'w_ple_gate': w_ple_gate, 'ple_norm': ple_norm}


def reference(x_prompt, x_sample, p_prompt, p_sample, norm_pre, w_in, b_gate, w_pool,
              pool_scale, mlstm_norm, q_norm, k_norm, w_out, norm_post, w_ple_proj,
              w_ple_gate, ple_norm):
    y_prompt = trunk(x_prompt, p_prompt, norm_pre, w_in, b_gate, w_pool, pool_scale, mlstm_norm,
                     q_norm, k_norm, w_out, norm_post, w_ple_proj, w_ple_gate, ple_norm)
    y_sample = trunk(x_sample, p_sample, norm_pre, w_in, b_gate, w_pool, pool_scale, mlstm_norm,
                     q_norm, k_norm, w_out, norm_post, w_ple_proj, w_ple_gate, ple_norm)
    return (y_prompt, y_sample)
```

```python
import contextlib
import numpy as np
import concourse.bass as bass
import concourse.mybir as mybir
from concourse.bass_utils import run_bass_kernel_spmd

F32 = mybir.dt.float32
BF16 = mybir.dt.bfloat16
AF = mybir.ActivationFunctionType
ALU = mybir.AluOpType

D = 4096
KC = 32
IN_W = 12304
EPS = 1e-6
O_PU, O_PZ, O_MQ, O_MK, O_MV, O_MO, O_MZ, O_MG, O_AQ, O_AK, O_AV, O_AZ = (
    0, 1024, 2048, 3072, 4096, 5120, 6144, 7168, 7184, 9232, 9744, 10256)
FC_PZ, FC_MQ, FC_MK, FC_MO, FC_MZ, FC_AQ, FC_AK, FC_AZ, NFC = 0, 8, 16, 24, 32, 40, 56, 60, 76
TOKW = 2560


class Prog:
    ENGS = ["pe", "act", "dve", "pool", "sp"]

    def __init__(self, nc, es, needs=None, n_dma_sems=40):
        self.nc = nc
        self.dry = needs is None
        self.needs = needs
        self.needs_out = []
        self.n = 0
        self.nds = n_dma_sems
        self.eng = {"pe": nc.tensor, "act": nc.scalar, "dve": nc.vector,
                    "pool": nc.gpsimd, "sp": nc.sync}
        if not self.dry:
            self.esem = {e: es.enter_context(nc.semaphore("es_" + e)) for e in self.ENGS}
            self.dsem = [es.enter_context(nc.semaphore("ds%d" % i)) for i in range(n_dma_sems)]
        self.n_sw = 2
        self.rr_sw = 0
        self.last_w = {}
        self.rd_c = {}
        self.rd_d = {}
        self.last_eng = {}
        self.dma_prev = [None] * n_dma_sems
        self.rr = 0
        self.ev = {}
        self.seq = {e: 0 for e in self.ENGS}
        self.dcount = [0] * n_dma_sems
        self.waited_c = {e: {x: 0 for x in self.ENGS} for e in self.ENGS}
        self.waited_d = {e: [0] * n_dma_sems for e in self.ENGS}

    def _mark(self, d):
        if self.dry:
            no = self.needs_out
            for j in d:
                no[j] = 1

    def _waits(self, E, dset):
        h = self.eng[E]
        for j in sorted(dset):
            kind, who, val = self.ev[j]
            if kind == "d":
                if self.waited_d[E][who] < val:
                    h.wait_ge(self.dsem[who], val)
                    self.waited_d[E][who] = val
            else:
                if who == E and E == "pe":
                    continue
                if self.waited_c[E][who] < val:
                    h.wait_ge(self.esem[who], val)
                    self.waited_c[E][who] = val

    def add(self, eng, emit, reads=(), writes=(), dma=False):
        i = self.n
        self.n += 1
        if self.dry:
            self.needs_out.append(1 if dma else 0)
        if eng in ("act", "dve"):
            xs_ = tuple("x" + r for r in reads if r.startswith("ps"))
            if xs_:
                writes = tuple(writes) + xs_
        d = set()
        last_w = self.last_w
        for r in reads:
            j = last_w.get(r)
            if j is not None:
                d.add(j)
        for w in writes:
            j = last_w.get(w)
            if j is not None:
                d.add(j)
            rc = self.rd_c.get(w)
            if rc:
                d.update(rc.values())
            rdl = self.rd_d.get(w)
            if rdl:
                d.update(rdl)
        k = None
        if dma:
            if eng == "pool":
                k = self.nds - self.n_sw + (self.rr_sw % self.n_sw)
                self.rr_sw += 1
            else:
                k = self.rr % (self.nds - self.n_sw)
                self.rr += 1
            if self.dma_prev[k] is not None:
                d.add(self.dma_prev[k])
            self.dma_prev[k] = i
        self._mark(d)
        for r in reads:
            if dma:
                self.rd_d.setdefault(r, []).append(i)
            else:
                self.rd_c.setdefault(r, {})[eng] = i
        for w in writes:
            last_w[w] = i
            if w in self.rd_c:
                self.rd_c[w] = {}
            if w in self.rd_d:
                self.rd_d[w] = []
        if not dma:
            self.last_eng[eng] = i
        if self.dry:
            return
        self._waits(eng, d)
        ins = emit()
        if dma:
            self.dcount[k] += 16
            ins.then_inc(self.dsem[k], 16)
            self.ev[i] = ("d", k, self.dcount[k])
        elif self.needs[i]:
            self.seq[eng] += 1
            ins.then_inc(self.esem[eng], 1)
            self.ev[i] = ("c", eng, self.seq[eng])
        else:
            self.ev[i] = ("c", eng, self.seq[eng] + 1)

    def barrier(self):
        i = self.n
        self.n += 1
        d = set(self.last_eng.values())
        for k in range(self.nds):
            if self.dma_prev[k] is not None:
                d.add(self.dma_prev[k])
        if self.dry:
            self.needs_out.append(1)
        self._mark(d)
        self.last_w.clear(); self.rd_c.clear(); self.rd_d.clear()
        self.last_eng = {}
        if self.dry:
            return
        self._waits("sp", d)
        ins = self.nc.sync.nop()
        self.seq["sp"] += 1
        ins.then_inc(self.esem["sp"], 1)
        self.ev = {i: ("c", "sp", self.seq["sp"])}
        for e in ("pe", "act", "dve", "pool"):
            self.eng[e].wait_ge(self.esem["sp"], self.seq["sp"])
            self.waited_c[e]["sp"] = self.seq["sp"]
        for k in range(self.nds):
            if self.dma_prev[k] is not None:
                self.ev[self.dma_prev[k]] = ("d", k, self.dcount[k])

    def finalize(self):
        if self.dry:
            return
        for k in range(self.nds):
            if self.dcount[k] > self.waited_d["sp"][k]:
                self.nc.sync.wait_ge(self.dsem[k], self.dcount[k])
        for e in ("pe", "act", "dve", "pool"):
            if self.seq[e] > self.waited_c["sp"][e]:
                self.nc.sync.wait_ge(self.esem[e], self.seq[e])


def build(T, NSEG, L):
    needs = _build(T, NSEG, L, None)
    return _build(T, NSEG, L, needs)


class _Stop(Exception):
    pass


_UN = [0]


def _un(name):
    _UN[0] += 1
    return "%s_u%d" % (name, _UN[0])


import os as _os


def _build(T, NSEG, L, needs):
    STOP = _os.environ.get("KSTOP", "")
    SEG = T // NSEG
    NC_ = T // 64
    NT128 = T // 128
    nc = bass.Bass("TRN2", target_bir_lowering=False)
    dt_in = lambda name, shape: nc.dram_tensor(name, shape, F32, kind="ExternalInput").ap()
    x_in = dt_in("x", [T, D])
    p_in = dt_in("p", [L, T, 256])
    norm_pre = dt_in("norm_pre", [L, D]); w_in = dt_in("w_in", [L, D, IN_W])
    b_gate = dt_in("b_gate", [L, 16]); w_pool = dt_in("w_pool", [L, 4, 256, 256])
    pool_scale = dt_in("pool_scale", [L, 1024]); mlstm_norm = dt_in("mlstm_norm", [L, 1024])
    q_norm = dt_in("q_norm", [L, 128]); k_norm = dt_in("k_norm", [L, 128])
    w_out = dt_in("w_out", [L, D, D]); norm_post = dt_in("norm_post", [L, D])
    w_ple_proj = dt_in("w_ple_proj", [L, 256, D]); w_ple_gate = dt_in("w_ple_gate", [L, D, D])
    ple_norm = dt_in("ple_norm", [L, D])
    ropec = dt_in("ropec", [128, T]); ropes = dt_in("ropes", [128, T])
    invcnt = dt_in("invcnt", [4, T]); flags = dt_in("flags", [128, 8])
    cmat = dt_in("cmat", [128, 8, 128])
    out = nc.dram_tensor("out", [T, D], F32, kind="ExternalOutput").ap()

    def scr(name, shape, dt):
        return nc.dram_tensor(name, shape, dt, kind="Internal").ap()
    wcols = ([O_PU + 128 * i for i in range(8)] + [O_PZ + 128 * i for i in range(8)] +
             [O_MQ + 128 * i for i in range(8)] + [O_MK + 128 * i for i in range(8)] +
             [O_MV + 128 * i for i in range(8)] + [O_MO + 128 * i for i in range(8)] +
             [O_MZ + 128 * i for i in range(8)] + [O_AQ + 128 * i for i in range(16)] +
             [O_AK + 128 * i for i in range(4)] + [O_AV + 128 * i for i in range(4)] +
             [O_AZ + 128 * i for i in range(16)])
    wid_of = {c: i for i, c in enumerate(wcols)}
    NWC = len(wcols)
    win_bf = scr("win_bf", [L, NWC, 128, KC * 128], BF16)
    wg_bf = scr("wg_bf", [L, 128, KC * 16], BF16)
    wout_bf = scr("wout_bf", [L, 32, 128, KC * 128], BF16)
    wgate_bf = scr("wgate_bf", [L, 32, 128, KC * 128], BF16)
    wproj_bf = scr("wproj_bf", [L, 32, 128, 2 * 128], BF16)
    wpool_bf = scr("wpool_bf", [L, 4, 2, 128, 2 * 128], BF16)
    xT = scr("xT", [32, 128, T], F32)
    hT = scr("hT", [32, 128, T], BF16)
    ppT = scr("ppT", [L, 2, 128, T], BF16)
    uT = scr("uT", [8, 128, T], F32)
    featT = scr("featT", [NFC, 128, T], BF16)
    tokmaj = scr("tokmaj", [T, TOKW], BF16)
    gates_s = scr("gates_s", [T, 16], F32)
    hfb = scr("hfb", [2, T, 1024], F32)
    ycT = scr("ycT", [32, 128, T], BF16)

    es = contextlib.ExitStack()
    with es:
        P = Prog(nc, es, needs)
        A = P.add
        sb = lambda name, shape, dt: es.enter_context(nc.sbuf_tensor(_un(name), shape, dt))
        ps = [es.enter_context(nc.psum_tensor("ps%d" % i, [128, 512], F32)) for i in range(8)]
        cm = sb("cm", [128, 8, 128], F32)
        cmb = sb("cmb", [128, 8, 128], BF16)
        flg = sb("flg", [128, 8], F32)
        vecs = sb("vecs", [128, L, 5, 32], F32)
        qkn = sb("qkn", [128, L, 2], F32)
        IDENT, RMAT, TRIF, TRIB, MASKF, MASKB, ONES = range(7)
        A("sp", lambda: nc.sync.dma_start(out=cm[:], in_=cmat), writes=["cm"], dma=True)
        A("sp", lambda: nc.sync.dma_start(out=flg[:], in_=flags), writes=["flg"], dma=True)
        for l in range(L):
            for vi, src in enumerate((norm_pre, norm_post, ple_norm)):
                A("sp", lambda l=l, vi=vi, src=src: nc.sync.dma_start(
                    out=vecs[:, l, vi, :], in_=src[l].rearrange("(c p) -> p c", p=128),
                    allow_slow_non_contiguous=True), writes=["vecs"], dma=True)
            for vi, src in ((3, pool_scale), (4, mlstm_norm)):
                A("sp", lambda l=l, vi=vi, src=src: nc.sync.dma_start(
                    out=vecs[:, l, vi, 0:8], in_=src[l].rearrange("(c p) -> p c", p=128),
                    allow_slow_non_contiguous=True), writes=["vecs"], dma=True)
            for vi, src in ((0, q_norm), (1, k_norm)):
                A("sp", lambda l=l, vi=vi, src=src: nc.sync.dma_start(
                    out=qkn[:, l, vi:vi + 1], in_=src[l].rearrange("(p o) -> p o", o=1),
                    allow_slow_non_contiguous=True), writes=["qkn"], dma=True)
        A("dve", lambda: nc.vector.tensor_copy(out=cmb[:], in_=cm[:]), reads=["cm"], writes=["cmb"])

        try:
            def conv(dst, src, kcn):
                A("pool", lambda: nc.gpsimd.dma_start(
                    out=dst.rearrange("p (kc j) -> p kc j", kc=kcn),
                    in_=src.rearrange("(kc p) j -> p kc j", p=128)),
                  dma=True)
            for l in range(L):
                for i, c0 in enumerate(wcols):
                    conv(win_bf[l, i], w_in[l][:, c0:c0 + 128], KC)
                conv(wg_bf[l], w_in[l][:, O_MG:O_MG + 16], KC)
                for i in range(32):
                    conv(wout_bf[l, i], w_out[l][:, i * 128:(i + 1) * 128], KC)
                    conv(wgate_bf[l, i], w_ple_gate[l][:, i * 128:(i + 1) * 128], KC)
                    conv(wproj_bf[l, i], w_ple_proj[l][:, i * 128:(i + 1) * 128], 2)
                for g in range(4):
                    for dd in range(2):
                        conv(wpool_bf[l, g, dd], w_pool[l, g][:, dd * 128:(dd + 1) * 128], 2)

            with contextlib.ExitStack() as e1:
                sb1 = lambda name, shape, dt: e1.enter_context(nc.sbuf_tensor(_un(name), shape, dt))
                xr = [sb1("xr%d" % i, [128, D], F32) for i in range(2)]
                xs = [sb1("xs%d" % i, [128, D], F32) for i in range(2)]
                pr = [sb1("pr%d" % i, [128, L * 256], F32) for i in range(2)]
                st = [sb1("st%d" % i, [128, 4], F32) for i in range(2)]
                xo = [sb1("xo%d" % i, [128, 4, 128], F32) for i in range(4)]
                ho = [sb1("ho%d" % i, [128, 4, 128], BF16) for i in range(4)]
                junk = sb1("junk1", [128, D], BF16)
                cnt = 0
                for it in range(NT128):
                    s = it % 2
                    r0 = it * 128
                    A("sp", lambda s=s, r0=r0: nc.sync.dma_start(out=xr[s][:], in_=x_in[r0:r0 + 128, :]),
                      writes=["xr%d" % s], dma=True)
                    A("sp", lambda s=s, r0=r0: nc.sync.dma_start(
                        out=pr[s][:].rearrange("p (l f) -> p l f", l=L),
                        in_=p_in[:, r0:r0 + 128, :].rearrange("l p f -> p l f")),
                      writes=["pr%d" % s], dma=True)
                    A("act", lambda s=s: nc.scalar.activation(out=junk[:], in_=xr[s][:], func=AF.Square,
                                                              accum_out=st[s][:, 0:1]),
                      reads=["xr%d" % s], writes=["junk1", "st%d" % s])
                    A("act", lambda s=s: nc.scalar.activation(out=st[s][:, 1:2], in_=st[s][:, 0:1], func=AF.Sqrt,
                                                              scale=1.0 / D, bias=cm[:, 7, 0:1]),
                      reads=["st%d" % s, "cm"], writes=["st%d" % s])
                    A("dve", lambda s=s: nc.vector.reciprocal(out=st[s][:, 2:3], in_=st[s][:, 1:2]),
                      reads=["st%d" % s], writes=["st%d" % s])
                    A("dve", lambda s=s: nc.vector.tensor_scalar(out=xs[s][:], in0=xr[s][:], scalar1=st[s][:, 2:3],
                                                                 scalar2=None, op0=ALU.mult),
                      reads=["xr%d" % s, "st%d" % s], writes=["xs%d" % s])
                    for q in range(8):
                        b0 = (cnt * 2) % 8; b1 = (cnt * 2 + 1) % 8; o = cnt % 4; cnt += 1
                        for j in range(4):
                            dc = q * 4 + j
                            A("pe", lambda s=s, b0=b0, j=j, dc=dc: nc.tensor.transpose(
                                out=ps[b0][:, j * 128:(j + 1) * 128], in_=xr[s][:, dc * 128:(dc + 1) * 128],
                                identity=cm[:, IDENT, :]), reads=["xr%d" % s, "cm"], writes=["ps%d" % b0])
                        for j in range(4):
                            dc = q * 4 + j
                            A("pe", lambda s=s, b1=b1, j=j, dc=dc: nc.tensor.transpose(
                                out=ps[b1][:, j * 128:(j + 1) * 128], in_=xs[s][:, dc * 128:(dc + 1) * 128],
                                identity=cm[:, IDENT, :]), reads=["xs%d" % s, "cm"], writes=["ps%d" % b1])
                        A("act", lambda b0=b0, o=o: nc.scalar.copy(out=xo[o][:].rearrange("p a b -> p (a b)"), in_=ps[b0][:]),
                          reads=["ps%d" % b0], writes=["xo%d" % o])
                        for j in range(4):
                            dc = q * 4 + j
                            A("dve", lambda b1=b1, o=o, j=j, dc=dc: nc.vector.tensor_scalar(
                                out=ho[o][:, j, :], in0=ps[b1][:, j * 128:(j + 1) * 128],
                                scalar1=vecs[:, 0, 0, dc:dc + 1], scalar2=None, op0=ALU.mult),
                              reads=["ps%d" % b1, "vecs"], writes=["ho%d" % o])
                        A("sp", lambda o=o, q=q, r0=r0: nc.sync.dma_start(
                            out=xT[q * 4:(q + 1) * 4, :, r0:r0 + 128].rearrange("c p t -> p c t"), in_=xo[o][:]),
                          reads=["xo%d" % o], dma=True)
                        A("sp", lambda o=o, q=q, r0=r0: nc.sync.dma_start(
                            out=hT[q * 4:(q + 1) * 4, :, r0:r0 + 128].rearrange("c p t -> p c t"), in_=ho[o][:]),
                          reads=["ho%d" % o], dma=True)
                    bq = (cnt * 2) % 8; o = cnt % 4; cnt += 1
                    for j in range(L * 2):
                        A("pe", lambda s=s, bq=bq, j=j: nc.tensor.transpose(
                            out=ps[bq][:, j * 128:(j + 1) * 128], in_=pr[s][:, j * 128:(j + 1) * 128],
                            identity=cm[:, IDENT, :]), reads=["pr%d" % s, "cm"], writes=["ps%d" % bq])
                    A("act", lambda bq=bq, o=o: nc.scalar.copy(
                        out=ho[o][:, 0:L * 2, :].rearrange("p a b -> p (a b)"), in_=ps[bq][:, 0:L * 256]),
                      reads=["ps%d" % bq], writes=["ho%d" % o])
                    A("sp", lambda o=o, r0=r0: nc.sync.dma_start(
                        out=ppT[:, :, :, r0:r0 + 128].rearrange("l c p t -> p (l c) t"), in_=ho[o][:, 0:L * 2, :]),
                      reads=["ho%d" % o], dma=True)
            P.barrier()
            if STOP == "p1":
                raise _Stop()

            for l in range(L):
                last = (l == L - 1)
                with contextlib.ExitStack() as ea:
                    sba = lambda name, shape, dt: ea.enter_context(nc.sbuf_tensor(_un(name), shape, dt))
                    TT = min(1024, T)
                    NH = TT // 512
                    hsb = sba("hsb", [128, KC, TT], BF16)
                    wt = [sba("wt%d" % i, [128, KC, 128], BF16) for i in range(3)]
                    wgt = sba("wgt", [128, KC, 16], BF16)
                    ob = [sba("ob%d" % i, [128, 512], BF16) for i in range(4)]
                    of = [sba("of%d" % i, [128, 512], F32) for i in range(2)]
                    cs = [sba("cs%d" % i, [128, 2, 512], F32) for i in range(NH)]
                    xf = sba("xf", [128, 512], BF16); sq = sba("sq", [128, 512], BF16)
                    lnb = sba("lnb", [128, 512], F32); rinv = sba("rinv", [128, 512], F32)
                    t1 = sba("t1", [128, 512], F32); t2 = sba("t2", [128, 512], F32)
                    gtb = sba("gtb", [128, 8, 16], F32)
                    rg = sba("rg", [128, 2, 128], BF16)
                    for vi in range(2):
                        A("dve", lambda vi=vi: nc.vector.tensor_scalar(
                            out=rg[:, vi, :], in0=cm[:, RMAT, :], scalar1=qkn[:, l, vi:vi + 1], scalar2=None,
                            op0=ALU.mult), reads=["cm", "qkn"], writes=["rg"])
                    A("sp", lambda: nc.sync.dma_start(out=wgt[:].rearrange("p a b -> p (a b)"), in_=wg_bf[l]),
                      writes=["wgt"], dma=True)
                    chunks = []
                    for i in range(8): chunks.append(("u", O_PU + 128 * i, i))
                    for i in range(8): chunks.append(("q", O_MQ + 128 * i, FC_MQ + i))
                    for i in range(8): chunks.append(("k", O_MK + 128 * i, FC_MK + i))
                    for i in range(8): chunks.append(("T", O_MK + 128 * i, i * 128))
                    for i in range(8): chunks.append(("T", O_MV + 128 * i, 1024 + i * 128))
                    for i in range(4): chunks.append(("T", O_AV + 128 * i, 2048 + i * 128))
                    chunks.append(("G", None, None))
                    for i in range(16): chunks.append(("rq", O_AQ + 128 * i, FC_AQ + i))
                    for i in range(4): chunks.append(("rk", O_AK + 128 * i, FC_AK + i))
                    for i in range(8): chunks.append(("silu", O_PZ + 128 * i, FC_PZ + i))
                    for i in range(8): chunks.append(("silu", O_MZ + 128 * i, FC_MZ + i))
                    for i in range(16): chunks.append(("silu", O_AZ + 128 * i, FC_AZ + i))
                    for i in range(8): chunks.append(("sig", O_MO + 128 * i, FC_MO + i))
                    _kk = _os.environ.get("KKINDS", "")
                    if _kk:
                        chunks = [c for c in chunks if c[0] in _kk.split(",")]
                    wl = [c for c in chunks if c[0] != "G"]
                    gcount = 0
                    for tt in range(T // TT):
                        t0 = tt * TT
                        A("sp", lambda t0=t0: nc.sync.dma_start(
                            out=hsb[:], in_=hT[:, :, t0:t0 + TT].rearrange("c p t -> p c t")),
                          writes=["hsb"], dma=True)
                        for hh in range(NH):
                            A("sp", lambda hh=hh, t0=t0: nc.sync.dma_start(
                                out=cs[hh][:, 0, :], in_=ropec[:, t0 + hh * 512:t0 + (hh + 1) * 512]),
                              writes=["cs%d" % hh], dma=True)
                            A("sp", lambda hh=hh, t0=t0: nc.sync.dma_start(
                                out=cs[hh][:, 1, :], in_=ropes[:, t0 + hh * 512:t0 + (hh + 1) * 512]),
                              writes=["cs%d" % hh], dma=True)
                        wi = 0
                        def wload(k, wi):
                            slot = k % 3
                            c0 = wl[wi][1]
                            A("sp", lambda slot=slot, c0=c0: nc.sync.dma_start(
                                out=wt[slot][:].rearrange("p a b -> p (a b)"), in_=win_bf[l, wid_of[c0]]),
                              writes=["wt%d" % slot], dma=True)
                        wk_ = tt * len(wl)
                        wload(wk_, 0); wload(wk_ + 1, 1)
                        for (kind, c0, dst) in chunks:
                            if kind == "G":
                                b = gcount % 2; gcount += 1
                                for sub in range(TT // 128):
                                    for kc in range(KC):
                                        A("pe", lambda b=b, sub=sub, kc=kc: nc.tensor.matmul(
                                            ps[b][:, sub * 16:(sub + 1) * 16], lhsT=hsb[:, kc, sub * 128:(sub + 1) * 128],
                                            rhs=wgt[:, kc, :], start=(kc == 0), stop=(kc == KC - 1)),
                                          reads=["hsb", "wgt"], writes=["ps%d" % b])
                                A("dve", lambda b=b: nc.vector.tensor_copy(
                                    out=gtb[:, 0:TT // 128, :].rearrange("p a b -> p (a b)"), in_=ps[b][:, 0:(TT // 128) * 16]),
                                  reads=["ps%d" % b], writes=["gtb"])
                                A("sp", lambda t0=t0: nc.sync.dma_start(
                                    out=gates_s[t0:t0 + TT, :].rearrange("(s p) g -> p s g", p=128),
                                    in_=gtb[:, 0:TT // 128, :]), reads=["gtb"], dma=True)
                                continue
                            slot = (wk_ + wi) % 3
                            if wi + 2 < len(wl):
                                wload(wk_ + wi + 2, wi + 2)
                            wi += 1
                            wkey = "wt%d" % slot
                            if kind == "T":
                                for h4 in range(TT // 512):
                                    b = gcount % 2; gcount += 1
                                    o = gcount % 4
                                    for s4 in range(4):
                                        sub = h4 * 4 + s4
                                        for kc in range(KC):
                                            A("pe", lambda b=b, s4=s4, sub=sub, kc=kc, slot=slot: nc.tensor.matmul(
                                                ps[b][:, s4 * 128:(s4 + 1) * 128],
                                                lhsT=hsb[:, kc, sub * 128:(sub + 1) * 128], rhs=wt[slot][:, kc, :],
                                                start=(kc == 0), stop=(kc == KC - 1)),
                                              reads=["hsb", wkey], writes=["ps%d" % b])
                                    A("act", lambda b=b, o=o: nc.scalar.copy(out=ob[o][:], in_=ps[b][:]),
                                      reads=["ps%d" % b], writes=["ob%d" % o])
                                    A("sp", lambda o=o, t0=t0, h4=h4, dst=dst: nc.sync.dma_start(
                                        out=tokmaj[t0 + h4 * 512:t0 + (h4 + 1) * 512, dst:dst + 128].rearrange(
                                            "(s p) c -> p s c", p=128),
                                        in_=ob[o][:].rearrange("p (s c) -> p s c", s=4)),
                                      reads=["ob%d" % o], dma=True)
                                continue
                            for hh in range(NH):
                                b = gcount % 2; gcount += 1
                                o = gcount % 4
                                for kc in range(KC):
                                    A("pe", lambda b=b, hh=hh, kc=kc, slot=slot: nc.tensor.matmul(
                                        ps[b][:], lhsT=wt[slot][:, kc, :], rhs=hsb[:, kc, hh * 512:(hh + 1) * 512],
                                        start=(kc == 0), stop=(kc == KC - 1)),
                                      reads=["hsb", wkey], writes=["ps%d" % b])
                                tsl = slice(t0 + hh * 512, t0 + (hh + 1) * 512)
                                pk = "ps%d" % b
                                if kind == "u":
                                    of_i = gcount % 2
                                    A("act", lambda b=b, of_i=of_i: nc.scalar.copy(out=of[of_i][:], in_=ps[b][:]),
                                      reads=[pk], writes=["of%d" % of_i])
                                    A("sp", lambda of_i=of_i, dst=dst, tsl=tsl: nc.sync.dma_start(
                                        out=uT[dst, :, tsl], in_=of[of_i][:]),
                                      reads=["of%d" % of_i], dma=True)
                                    continue
                                if kind in ("q", "k"):
                                    sc = 1.0 / 16.0 if kind == "q" else 1.0
                                    A("dve", lambda b=b, o=o, sc=sc: nc.vector.tensor_scalar(
                                        out=ob[o][:], in0=ps[b][:], scalar1=sc, scalar2=None, op0=ALU.mult),
                                      reads=[pk], writes=["ob%d" % o])
                                elif kind in ("silu", "sig"):
                                    fn = AF.Silu if kind == "silu" else AF.Sigmoid
                                    A("act", lambda b=b, o=o, fn=fn: nc.scalar.activation(out=ob[o][:], in_=ps[b][:], func=fn),
                                      reads=[pk], writes=["ob%d" % o])
                                else:
                                    vi = 0 if kind == "rq" else 1
                                    A("dve", lambda b=b: nc.vector.tensor_copy(out=xf[:], in_=ps[b][:]),
                                      reads=[pk], writes=["xf"])
                                    A("act", lambda b=b: nc.scalar.activation(out=sq[:], in_=ps[b][:], func=AF.Square),
                                      reads=[pk], writes=["sq"])
                                    A("pe", lambda: nc.tensor.matmul(ps[2][:], lhsT=cmb[:, ONES, :], rhs=sq[:],
                                                                     start=True, stop=True),
                                      reads=["cmb", "sq"], writes=["ps2"])
                                    A("pe", lambda vi=vi: nc.tensor.matmul(ps[3][:], lhsT=rg[:, vi, :], rhs=xf[:],
                                                                           start=True, stop=True),
                                      reads=["rg", "xf"], writes=["ps3"])
                                    A("act", lambda: nc.scalar.activation(out=lnb[:], in_=ps[2][:], func=AF.Ln,
                                                                          scale=1.0 / 128.0, bias=cm[:, 7, 0:1]),
                                      reads=["ps2", "cm"], writes=["lnb"])
                                    A("act", lambda: nc.scalar.activation(out=rinv[:], in_=lnb[:], func=AF.Exp, scale=-0.5),
                                      reads=["lnb"], writes=["rinv"])
                                    A("dve", lambda hh=hh, vi=vi, b=b: nc.vector.scalar_tensor_tensor(
                                        out=t1[:], in0=ps[b][:], scalar=qkn[:, l, vi:vi + 1], in1=cs[hh][:, 0, :],
                                        op0=ALU.mult, op1=ALU.mult), reads=[pk, "qkn", "cs%d" % hh], writes=["t1"])
                                    A("dve", lambda hh=hh: nc.vector.tensor_tensor(
                                        out=t2[:], in0=ps[3][:], in1=cs[hh][:, 1, :], op=ALU.mult),
                                      reads=["ps3", "cs%d" % hh], writes=["t2"])
                                    A("pool", lambda: nc.gpsimd.tensor_tensor(out=t1[:], in0=t1[:], in1=t2[:], op=ALU.add),
                                      reads=["t1", "t2"], writes=["t1"])
                                    A("dve", lambda o=o: nc.vector.tensor_tensor(out=ob[o][:], in0=t1[:], in1=rinv[:],
                                                                                 op=ALU.mult),
                                      reads=["t1", "rinv"], writes=["ob%d" % o])
                                A("sp", lambda o=o, dst=dst, tsl=tsl: nc.sync.dma_start(
                                    out=featT[dst, :, tsl], in_=ob[o][:]),
                                  reads=["ob%d" % o], dma=True)
                P.barrier()
                if STOP == "pa":
                    raise _Stop()
                with contextlib.ExitStack() as eb:
                    sbb = lambda name, shape, dt: eb.enter_context(nc.sbuf_tensor(_un(name), shape, dt))
                    wp = sbb("wp", [128, 4, 2, 256], BF16)
                    A("sp", lambda: nc.sync.dma_start(out=wp[:].rearrange("p g d f -> p (g d) f"),
                                                      in_=wpool_bf[l].rearrange("g d p f -> p (g d) f")),
                      writes=["wp"], dma=True)
                    ub = [[sbb("ub%d_%d" % (i, cc), [128, 528], F32) for cc in range(2)] for i in range(2)]
                    sa = [sbb("sa%d" % cc, [128, 528], F32) for cc in range(2)]
                    sb_ = [sbb("sbb%d" % cc, [128, 528], F32) for cc in range(2)]
                    ic = [sbb("ic%d" % i, [128, 512], F32) for i in range(2)]
                    db = [[sbb("db%d_%d" % (i, cc), [128, 512], BF16) for cc in range(2)] for i in range(2)]
                    zb = [[sbb("zb%d_%d" % (i, dd), [128, 512], BF16) for dd in range(2)] for i in range(2)]
                    yo = [sbb("yo%d" % i, [128, 512], BF16) for i in range(4)]
                    tmp = sbb("ptmp", [128, 512], F32)
                    it = 0
                    for g in range(4):
                        w = (2, 4, 8, 16)[g]
                        for tl in range(T // 512):
                            t0 = tl * 512
                            s = it % 2; it += 1
                            lo = t0 - 8; hi = t0 + 520
                            segstart = (t0 % SEG == 0); segend = ((t0 + 512) % SEG == 0)
                            for cc in range(2):
                                key = "ub%d_%d" % (s, cc)
                                u = ub[s][cc]
                                fc = g * 2 + cc
                                a0 = 8 if t0 == 0 else 0
                                a1 = 520 if t0 + 512 == T else 528
                                if a0 or a1 != 528:
                                    A("pool", lambda u=u: nc.gpsimd.memset(u[:], 0.0), writes=[key])
                                A("sp", lambda u=u, fc=fc, a0=a0, a1=a1, lo=lo: nc.sync.dma_start(
                                    out=u[:, a0:a1], in_=uT[fc, :, lo + a0:lo + a1]), writes=[key], dma=True)
                                if segstart and t0 != 0:
                                    A("pool", lambda u=u: nc.gpsimd.tensor_scalar(
                                        out=u[:, 0:8], in0=u[:, 0:8], scalar1=flg[:, 0:1], scalar2=None, op0=ALU.mult),
                                      reads=[key, "flg"], writes=[key])
                                if segend and t0 + 512 != T:
                                    A("pool", lambda u=u: nc.gpsimd.tensor_scalar(
                                        out=u[:, 520:528], in0=u[:, 520:528], scalar1=flg[:, 0:1], scalar2=None,
                                        op0=ALU.mult), reads=[key, "flg"], writes=[key])
                            A("sp", lambda s=s, g=g, t0=t0: nc.sync.dma_start(
                                out=ic[s][:], in_=invcnt[g:g + 1, t0:t0 + 512].partition_broadcast(128)),
                              writes=["ic%d" % s], dma=True)
                            for dd in range(2):
                                A("sp", lambda s=s, dd=dd, g=g, t0=t0: nc.sync.dma_start(
                                    out=zb[s][dd][:], in_=featT[FC_PZ + g * 2 + dd, :, t0:t0 + 512]),
                                  writes=["zb%d_%d" % (s, dd)], dma=True)
                            for cc in range(2):
                                u = ub[s][cc]; ukey = "ub%d_%d" % (s, cc)
                                eng = "dve" if cc == 0 else "pool"
                                E = nc.vector if cc == 0 else nc.gpsimd
                                cur = u; ckey = ukey
                                steps = [(1, 0)]
                                if w >= 4: steps.append((1, -1))
                                if w >= 8: steps.append((2, -2))
                                if w >= 16: steps.append((4, -4))
                                bufs = [sa[cc], sb_[cc]]; bkeys = ["sa%d" % cc, "sbb%d" % cc]
                                margin = 0
                                for si, (sh_l, sh_r) in enumerate(steps):
                                    dst_ = bufs[si % 2]; dkey = bkeys[si % 2]
                                    if si == 0:
                                        c_lo, c_hi = 1, 528
                                        A(eng, lambda E=E, dst_=dst_, cur=cur, c_lo=c_lo, c_hi=c_hi: E.tensor_tensor(
                                            out=dst_[:, c_lo:c_hi], in0=cur[:, c_lo - 1:c_hi - 1], in1=cur[:, c_lo:c_hi],
                                            op=ALU.add), reads=[ckey], writes=[dkey])
                                        vlo, vhi = 1, 528
                                    else:
                                        k = sh_l
                                        c_lo, c_hi = vlo + k, vhi - k
                                        A(eng, lambda E=E, dst_=dst_, cur=cur, c_lo=c_lo, c_hi=c_hi, k=k: E.tensor_tensor(
                                            out=dst_[:, c_lo:c_hi], in0=cur[:, c_lo - k:c_hi - k], in1=cur[:, c_lo + k:c_hi + k],
                                            op=ALU.add), reads=[ckey], writes=[dkey])
                                        vlo, vhi = c_lo, c_hi
                                    cur = dst_; ckey = dkey
                                assert vlo <= 8 and vhi >= 520
                                A(eng, lambda E=E, cur=cur, s=s: E.tensor_tensor(
                                    out=tmp[:] if False else cur[:, 8:520], in0=cur[:, 8:520], in1=ic[s][:], op=ALU.mult),
                                  reads=[ckey, "ic%d" % s], writes=[ckey])
                                A(eng, lambda E=E, cur=cur, s=s, cc=cc, u=u: E.tensor_tensor(
                                    out=db[s][cc][:], in0=cur[:, 8:520], in1=u[:, 8:520], op=ALU.subtract),
                                  reads=[ckey, ukey], writes=["db%d_%d" % (s, cc)])
                            for dd in range(2):
                                b = 4 + (it * 2 + dd) % 4
                                o = (it * 2 + dd) % 4
                                for cc in range(2):
                                    A("pe", lambda b=b, g=g, dd=dd, cc=cc, s=s: nc.tensor.matmul(
                                        ps[b][:], lhsT=wp[:, g, dd, cc * 128:(cc + 1) * 128], rhs=db[s][cc][:],
                                        start=(cc == 0), stop=(cc == 1)),
                                      reads=["wp", "db%d_%d" % (s, cc)], writes=["ps%d" % b])
                                A("dve", lambda b=b, o=o, g=g, dd=dd, s=s: nc.vector.scalar_tensor_tensor(
                                    out=yo[o][:], in0=ps[b][:], scalar=vecs[:, l, 3, g * 2 + dd:g * 2 + dd + 1],
                                    in1=zb[s][dd][:], op0=ALU.mult, op1=ALU.mult),
                                  reads=["ps%d" % b, "vecs", "zb%d_%d" % (s, dd)], writes=["yo%d" % o])
                                A("sp", lambda o=o, g=g, dd=dd, t0=t0: nc.sync.dma_start(
                                    out=ycT[g * 2 + dd, :, t0:t0 + 512], in_=yo[o][:]),
                                  reads=["yo%d" % o], dma=True)
                P.barrier()
                if STOP == "pb1":
                    raise _Stop()
                build_mlstm(nc, P, l, T, NSEG, ps, cm, cmb, flg, vecs, featT, tokmaj, gates_s, hfb, ycT, b_gate)
                P.barrier()
                if STOP == "pb2":
                    raise _Stop()
                with contextlib.ExitStack() as ec:
                    sbc = lambda name, shape, dt: ec.enter_context(nc.sbuf_tensor(_un(name), shape, dt))
                    kT = sbc("kTs", [128, T], BF16)
                    vS = sbc("vS", [128, NT128, 128], BF16)
                    qS = [sbc("qS%d" % i, [128, 512], BF16) for i in range(2)]
                    zS = [sbc("zS%d" % i, [128, 512], BF16) for i in range(2)]
                    pT = [sbc("pT%d" % i, [128, 512], BF16) for i in range(3)]
                    rd = sbc("rd", [128, 512], F32)
                    o1 = sbc("o1", [128, 512], F32)
                    yo = [sbc("ayo%d" % i, [128, 512], BF16) for i in range(2)]
                    itq = 0
                    NKT = T // 128
                    for g in range(4):
                        A("sp", lambda g=g: nc.sync.dma_start(out=kT[:], in_=featT[FC_AK + g]),
                          writes=["kTs"], dma=True)
                        A("sp", lambda g=g: nc.sync.dma_start(
                            out=vS[:], in_=tokmaj[:, 2048 + g * 128:2048 + (g + 1) * 128].rearrange("(n p) c -> p n c", p=128)),
                          writes=["vS"], dma=True)
                        for hq in range(4):
                            hd = g * 4 + hq
                            for qt in range(T // 512):
                                s = itq % 2; itq += 1
                                q0 = qt * 512
                                qseg = q0 // SEG
                                A("sp", lambda s=s, hd=hd, q0=q0: nc.sync.dma_start(
                                    out=qS[s][:], in_=featT[FC_AQ + hd, :, q0:q0 + 512]),
                                  writes=["qS%d" % s], dma=True)
                                A("sp", lambda s=s, hd=hd, q0=q0: nc.sync.dma_start(
                                    out=zS[s][:], in_=featT[FC_AZ + hd, :, q0:q0 + 512]),
                                  writes=["zS%d" % s], dma=True)
                                bo = 4 + 2 * s; bd = 5 + 2 * s

                                def smm(kt, s=s):
                                    b = kt % 2
                                    A("pe", lambda: nc.tensor.matmul(ps[b][:], lhsT=kT[:, kt * 128:(kt + 1) * 128],
                                                                     rhs=qS[s][:], start=True, stop=True),
                                      reads=["kTs", "qS%d" % s], writes=["ps%d" % b])

                                def expo(kt, qseg=qseg):
                                    b = kt % 2; pi = kt % 3
                                    kseg = (kt * 128) // SEG
                                    col = 4 + (kseg * 2 + qseg if NSEG == 2 else 0)
                                    A("act", lambda: nc.scalar.activation(
                                        out=pT[pi][:], in_=ps[b][:], func=AF.Exp, scale=128.0 ** -0.5,
                                        bias=flg[:, col:col + 1]),
                                      reads=["ps%d" % b, "flg"], writes=["pT%d" % pi])

                                def pv(kt, bo=bo, bd=bd):
                                    pi = kt % 3
                                    A("pe", lambda: nc.tensor.matmul(ps[bo][:], lhsT=vS[:, kt, :], rhs=pT[pi][:],
                                                                     start=(kt == 0), stop=(kt == NKT - 1)),
                                      reads=["vS", "pT%d" % pi], writes=["ps%d" % bo])
                                    A("pe", lambda: nc.tensor.matmul(ps[bd][:], lhsT=cmb[:, ONES, :], rhs=pT[pi][:],
                                                                     start=(kt == 0), stop=(kt == NKT - 1)),
                                      reads=["cmb", "pT%d" % pi], writes=["ps%d" % bd])
                                smm(0)
                                for kt in range(NKT):
                                    if kt + 1 < NKT:
                                        smm(kt + 1)
                                    expo(kt)
                                    pv(kt)
                                A("dve", lambda bd=bd: nc.vector.reciprocal(out=rd[:], in_=ps[bd][:]),
                                  reads=["ps%d" % bd], writes=["rd"])
                                A("dve", lambda bo=bo: nc.vector.tensor_tensor(out=o1[:], in0=ps[bo][:], in1=rd[:], op=ALU.mult),
                                  reads=["ps%d" % bo, "rd"], writes=["o1"])
                                A("pool", lambda s=s: nc.gpsimd.tensor_tensor(out=yo[s][:], in0=o1[:], in1=zS[s][:], op=ALU.mult),
                                  reads=["o1", "zS%d" % s], writes=["ayo%d" % s])
                                A("sp", lambda s=s, hd=hd, q0=q0: nc.sync.dma_start(
                                    out=ycT[16 + hd, :, q0:q0 + 512], in_=yo[s][:]),
                                  reads=["ayo%d" % s], dma=True)
                P.barrier()
                if STOP == "pb3":
                    raise _Stop()
                with contextlib.ExitStack() as ed:
                    sbd = lambda name, shape, dt: ed.enter_context(nc.sbuf_tensor(_un(name), shape, dt))
                    ycs = sbd("ycs", [128, KC, 512], BF16)
                    yb = sbd("yb", [128, KC, 512], F32)
                    wt = [sbd("cwt%d" % i, [128, KC, 128], BF16) for i in range(3)]
                    wpj = [sbd("wpj%d" % i, [128, 2, 128], BF16) for i in range(2)]
                    pTs = sbd("pTs", [128, 2, 512], BF16)
                    sqb = [sbd("sqb%d" % i, [128, 512], BF16) for i in range(2)]
                    xc = [sbd("xc%d" % i, [128, 512], F32) for i in range(3)]
                    x1o = [sbd("x1o%d" % i, [128, 512], F32) for i in range(2)]
                    sg = [sbd("sg%d" % i, [128, 512], F32) for i in range(2)]
                    lnb = sbd("clnb", [128, 512], F32); rinv = sbd("crinv", [128, 512], F32)
                    tq = sbd("ctq", [128, 512], F32)
                    hob = [sbd("hob%d" % i, [128, 512], BF16) for i in range(2)]
                    otile = [sbd("otile%d" % i, [128, 1024], F32) for i in range(2)] if last else None
                    wcount = 0

                    def wload(src, slot):
                        A("sp", lambda: nc.sync.dma_start(out=wt[slot][:].rearrange("p a b -> p (a b)"), in_=src),
                          writes=["cwt%d" % slot], dma=True)

                    def norm_from(bank):
                        flush()
                        A("act", lambda: nc.scalar.activation(out=lnb[:], in_=ps[bank][:], func=AF.Ln,
                                                              scale=1.0 / D, bias=cm[:, 7, 0:1]),
                          reads=["ps%d" % bank, "cm"], writes=["clnb"])
                        A("act", lambda: nc.scalar.activation(out=rinv[:], in_=lnb[:], func=AF.Exp, scale=-0.5),
                          reads=["clnb"], writes=["crinv"])

                    pend = []

                    def flush():
                        while pend:
                            pend.pop(0)()

                    def sq_acc(src_ap, src_keys, idx, bank):
                        flush()
                        si = idx % 2
                        A("act", lambda: nc.scalar.activation(out=sqb[si][:], in_=src_ap, func=AF.Square),
                          reads=src_keys, writes=["sqb%d" % si])
                        pend.append(lambda: A("pe", lambda: nc.tensor.matmul(
                            ps[bank][:], lhsT=cmb[:, ONES, :], rhs=sqb[si][:], start=(idx == 0), stop=(idx == 31)),
                            reads=["cmb", "sqb%d" % si], writes=["ps%d" % bank]))
                    for tl in range(T // 512):
                        t0 = tl * 512
                        tsl = slice(t0, t0 + 512)
                        A("sp", lambda tsl=tsl: nc.sync.dma_start(out=ycs[:], in_=ycT[:, :, tsl].rearrange("c p t -> p c t")),
                          writes=["ycs"], dma=True)
                        A("sp", lambda tsl=tsl: nc.sync.dma_start(out=pTs[:], in_=ppT[l, :, :, tsl].rearrange("c p t -> p c t")),
                          writes=["pTs"], dma=True)
                        wload(wout_bf[l, 0], wcount % 3); wload(wout_bf[l, 1], (wcount + 1) % 3)
                        for oc in range(32):
                            slot = (wcount + oc) % 3
                            if oc + 2 < 32:
                                wload(wout_bf[l, oc + 2], (wcount + oc + 2) % 3)
                            b = oc % 2
                            for kc in range(KC):
                                A("pe", lambda b=b, kc=kc, slot=slot: nc.tensor.matmul(
                                    ps[b][:], lhsT=wt[slot][:, kc, :], rhs=ycs[:, kc, :], start=(kc == 0), stop=(kc == KC - 1)),
                                  reads=["cwt%d" % slot, "ycs"], writes=["ps%d" % b])
                            flush()
                            A("dve", lambda b=b, oc=oc: nc.vector.tensor_copy(out=yb[:, oc, :], in_=ps[b][:]),
                              reads=["ps%d" % b], writes=["yb%d" % oc])
                            sq_acc(ps[b][:], ["ps%d" % b], oc, 2)
                        wcount += 32
                        norm_from(2)
                        for dc in range(32):
                            xi = dc % 3; oi = dc % 2
                            A("sp", lambda xi=xi, dc=dc, tsl=tsl: nc.sync.dma_start(out=xc[xi][:], in_=xT[dc, :, tsl]),
                              reads=["xT%d" % dc], writes=["xc%d" % xi], dma=True)
                            A("dve", lambda dc=dc: nc.vector.scalar_tensor_tensor(
                                out=tq[:], in0=yb[:, dc, :], scalar=vecs[:, l, 1, dc:dc + 1], in1=rinv[:],
                                op0=ALU.mult, op1=ALU.mult), reads=["yb%d" % dc, "vecs", "crinv"], writes=["ctq"])
                            A("pool", lambda xi=xi, oi=oi: nc.gpsimd.tensor_tensor(out=x1o[oi][:], in0=tq[:], in1=xc[xi][:], op=ALU.add),
                              reads=["ctq", "xc%d" % xi], writes=["x1o%d" % oi])
                            A("act", lambda oi=oi, dc=dc: nc.scalar.copy(out=ycs[:, dc, :], in_=x1o[oi][:]),
                              reads=["x1o%d" % oi], writes=["ycs"])
                            A("sp", lambda oi=oi, dc=dc, tsl=tsl: nc.sync.dma_start(out=xT[dc, :, tsl], in_=x1o[oi][:]),
                              reads=["x1o%d" % oi], writes=["xT%d" % dc], dma=True)
                        wload(wgate_bf[l, 0], wcount % 3); wload(wgate_bf[l, 1], (wcount + 1) % 3)
                        for ecn in range(32):
                            slot = (wcount + ecn) % 3
                            if ecn + 2 < 32:
                                wload(wgate_bf[l, ecn + 2], (wcount + ecn + 2) % 3)
                            pj = ecn % 2
                            A("sp", lambda pj=pj, ecn=ecn: nc.sync.dma_start(
                                out=wpj[pj][:].rearrange("p a b -> p (a b)"), in_=wproj_bf[l, ecn]),
                              writes=["wpj%d" % pj], dma=True)
                            b = ecn % 2; b2 = 4 + ecn % 2
                            for kc in range(KC):
                                A("pe", lambda b=b, kc=kc, slot=slot: nc.tensor.matmul(
                                    ps[b][:], lhsT=wt[slot][:, kc, :], rhs=ycs[:, kc, :], start=(kc == 0), stop=(kc == KC - 1)),
                                  reads=["cwt%d" % slot, "ycs"], writes=["ps%d" % b])
                            for rc in range(2):
                                A("pe", lambda b2=b2, rc=rc, pj=pj: nc.tensor.matmul(
                                    ps[b2][:], lhsT=wpj[pj][:, rc, :], rhs=pTs[:, rc, :], start=(rc == 0), stop=(rc == 1)),
                                  reads=["wpj%d" % pj, "pTs"], writes=["ps%d" % b2])
                            flush()
                            si = ecn % 2
                            A("act", lambda b=b, si=si: nc.scalar.activation(out=sg[si][:], in_=ps[b][:], func=AF.Sigmoid),
                              reads=["ps%d" % b], writes=["sg%d" % si])
                            A("dve", lambda b2=b2, si=si, ecn=ecn: nc.vector.tensor_tensor(
                                out=yb[:, ecn, :], in0=ps[b2][:], in1=sg[si][:], op=ALU.mult),
                              reads=["ps%d" % b2, "sg%d" % si], writes=["yb%d" % ecn])
                            sq_acc(yb[:, ecn, :], ["yb%d" % ecn], ecn, 3)
                        wcount += 32
                        norm_from(3)
                        for dc in range(32):
                            xi = dc % 3
                            A("sp", lambda xi=xi, dc=dc, tsl=tsl: nc.sync.dma_start(out=xc[xi][:], in_=xT[dc, :, tsl]),
                              reads=["xT%d" % dc], writes=["xc%d" % xi], dma=True)
                            A("dve", lambda dc=dc: nc.vector.scalar_tensor_tensor(
                                out=tq[:], in0=yb[:, dc, :], scalar=vecs[:, l, 2, dc:dc + 1], in1=rinv[:],
                                op0=ALU.mult, op1=ALU.mult), reads=["yb%d" % dc, "vecs", "crinv"], writes=["ctq"])
                            A("pool", lambda xi=xi, dc=dc: nc.gpsimd.tensor_tensor(out=yb[:, dc, :], in0=tq[:], in1=xc[xi][:], op=ALU.add),
                              reads=["ctq", "xc%d" % xi], writes=["yb%d" % dc])
                            if not last:
                                A("sp", lambda dc=dc, tsl=tsl: nc.sync.dma_start(out=xT[dc, :, tsl], in_=yb[:, dc, :]),
                                  reads=["yb%d" % dc], writes=["xT%d" % dc], dma=True)
                                sq_acc(yb[:, dc, :], ["yb%d" % dc], dc, 2)
                        if not last:
                            norm_from(2)
                            for dc in range(32):
                                oi = dc % 2
                                A("dve", lambda dc=dc, oi=oi: nc.vector.scalar_tensor_tensor(
                                    out=hob[oi][:], in0=yb[:, dc, :], scalar=vecs[:, l + 1, 0, dc:dc + 1], in1=rinv[:],
                                    op0=ALU.mult, op1=ALU.mult), reads=["yb%d" % dc, "vecs", "crinv"], writes=["hob%d" % oi])
                                A("sp", lambda dc=dc, oi=oi, tsl=tsl: nc.sync.dma_start(out=hT[dc, :, tsl], in_=hob[oi][:]),
                                  reads=["hob%d" % oi], dma=True)
                        else:
                            tcnt = 0
                            for sub in range(4):
                                for q in range(4):
                                    oi = tcnt % 2; tcnt += 1
                                    for hb in range(2):
                                        b = 4 + (tcnt * 2 + hb) % 4
                                        for j in range(4):
                                            dc = q * 8 + hb * 4 + j
                                            A("pe", lambda b=b, j=j, dc=dc, sub=sub: nc.tensor.transpose(
                                                out=ps[b][:, j * 128:(j + 1) * 128], in_=yb[:, dc, sub * 128:(sub + 1) * 128],
                                                identity=cm[:, IDENT, :]), reads=["yb%d" % dc, "cm"], writes=["ps%d" % b])
                                        if hb:
                                            A("act", lambda b=b, oi=oi, hb=hb: nc.scalar.copy(
                                                out=otile[oi][:, hb * 512:(hb + 1) * 512], in_=ps[b][:]),
                                              reads=["ps%d" % b], writes=["otile%d" % oi])
                                        else:
                                            A("dve", lambda b=b, oi=oi, hb=hb: nc.vector.tensor_copy(
                                                out=otile[oi][:, hb * 512:(hb + 1) * 512], in_=ps[b][:]),
                                              reads=["ps%d" % b], writes=["otile%d" % oi])
                                    A("sp", lambda oi=oi, sub=sub, q=q, t0=t0: nc.sync.dma_start(
                                        out=out[t0 + sub * 128:t0 + (sub + 1) * 128, q * 1024:(q + 1) * 1024], in_=otile[oi][:]),
                                      reads=["otile%d" % oi], dma=True)
                P.barrier()
        except _Stop:
            P.barrier()
        P.finalize()
    if needs is None:
        return bytes(bytearray(P.needs_out))
    return nc


def build_mlstm(nc, P, l, T, NSEG, ps, cm, cmb, flg, vecs, featT, tokmaj, gates_s, hfb, ycT, b_gate):
    A = P.add
    IDENT, RMAT, TRIF, TRIB, MASKF, MASKB, ONES = range(7)
    NCK = T // 64
    BLK = min(16, NCK)
    NB = NCK // BLK
    BT = BLK * 64
    SEGC = NCK // NSEG
    with contextlib.ExitStack() as em:
        sbm = lambda name, shape, dt: em.enter_context(nc.sbuf_tensor(_un(name), shape, dt))
        Gc = sbm("Gc", [64, NCK, 16], F32)
        bg = sbm("bg", [64, 16], F32)
        pre = sbm("pre", [64, 16, NCK], F32)
        LF = sbm("LF", [64, 8, NCK], F32)
        LI = sbm("LI", [64, 8, NCK], F32)
        Bc = sbm("Bc", [64, 8, NCK], F32)
        WK = sbm("WK", [64, 8, NCK], F32)
        WK2 = sbm("WK2", [64, 8, NCK], F32)
        FL = sbm("FL", [64, 8, NCK], F32)
        FD = sbm("FD", [128, 8, NCK], F32)
        tmpg = sbm("tmpg", [64, 8, NCK], F32)
        A("sp", lambda: nc.sync.dma_start(out=Gc[:], in_=gates_s.rearrange("(c j) g -> j c g", j=64)),
          writes=["Gc"], dma=True)
        A("sp", lambda: nc.sync.dma_start(out=bg[:], in_=b_gate[l:l + 1, :].partition_broadcast(64)),
          writes=["bg"], dma=True)
        for j in range(16):
            A("dve", lambda j=j: nc.vector.tensor_scalar(out=pre[:, j, :], in0=Gc[:, :, j], scalar1=bg[:, j:j + 1],
                                                         scalar2=None, op0=ALU.add),
              reads=["Gc", "bg"], writes=["pre"])
        for d in range(2):
            fs = slice(4 + 8 * d, 8 + 8 * d); is_ = slice(8 * d, 8 * d + 4); cs_ = slice(4 * d, 4 * d + 4)
            A("act", lambda fs=fs, cs_=cs_: nc.scalar.activation(out=tmpg[:, cs_, :], in_=pre[:, fs, :], func=AF.Exp, scale=-1.0),
              reads=["pre"], writes=["tmpg"])
            A("act", lambda cs_=cs_: nc.scalar.activation(out=LF[:, cs_, :], in_=tmpg[:, cs_, :], func=AF.Ln,
                                                          bias=cm[0:64, 7, 1:2]),
              reads=["tmpg", "cm"], writes=["LF"])
            A("dve", lambda cs_=cs_: nc.vector.tensor_scalar(out=LF[:, cs_, :], in0=LF[:, cs_, :], scalar1=-1.0,
                                                             scalar2=None, op0=ALU.mult), reads=["LF"], writes=["LF"])
            A("dve", lambda is_=is_, cs_=cs_: nc.vector.tensor_copy(out=LI[:, cs_, :], in_=pre[:, is_, :]),
              reads=["pre"], writes=["LI"])
        NW = 4 * NCK
        LFh = sbm("LFh", [64, 8, NCK], BF16)
        LFl = sbm("LFl", [64, 8, NCK], BF16)
        A("dve", lambda: nc.vector.tensor_copy(out=LFh[:], in_=LF[:]), reads=["LF"], writes=["LFh"])
        A("dve", lambda: nc.vector.tensor_tensor(out=tmpg[:], in0=LF[:], in1=LFh[:], op=ALU.subtract),
          reads=["LF", "LFh"], writes=["tmpg"])
        A("dve", lambda: nc.vector.tensor_copy(out=LFl[:], in_=tmpg[:]), reads=["tmpg"], writes=["LFl"])
        for d in range(2):
            cs_ = slice(4 * d, 4 * d + 4)
            tri = TRIF if d == 0 else TRIB
            for c0 in range(0, NW, 512):
                c1 = min(NW, c0 + 512)
                lfh = LFh[:, cs_, :].rearrange("p a b -> p (a b)")[:, c0:c1]
                lfl = LFl[:, cs_, :].rearrange("p a b -> p (a b)")[:, c0:c1]
                for pi_, lfv in enumerate((lfh, lfl)):
                    A("pe", lambda tri=tri, lfv=lfv, c0=c0, c1=c1, pi_=pi_: nc.tensor.matmul(
                        ps[0][0:64, 0:c1 - c0], lhsT=cmb[0:64, tri, 0:64], rhs=lfv, start=(pi_ == 0), stop=(pi_ == 1)),
                      reads=["cmb", "LFh", "LFl"], writes=["ps0"])
                for pi_, lfv in enumerate((lfh, lfl)):
                    A("pe", lambda lfv=lfv, c0=c0, c1=c1, pi_=pi_: nc.tensor.matmul(
                        ps[1][:, 0:c1 - c0], lhsT=cmb[0:64, ONES, :], rhs=lfv, start=(pi_ == 0), stop=(pi_ == 1)),
                      reads=["cmb", "LFh", "LFl"], writes=["ps1"])
                bv = Bc[:, cs_, :].rearrange("p a b -> p (a b)")[:, c0:c1]
                fv = FD[:, cs_, :].rearrange("p a b -> p (a b)")[:, c0:c1]
                liv = LI[:, cs_, :].rearrange("p a b -> p (a b)")[:, c0:c1]
                wkv = WK[:, cs_, :].rearrange("p a b -> p (a b)")[:, c0:c1]
                wk2v = WK2[:, cs_, :].rearrange("p a b -> p (a b)")[:, c0:c1]
                flv = FL[:, cs_, :].rearrange("p a b -> p (a b)")[:, c0:c1]
                tv = tmpg[:, cs_, :].rearrange("p a b -> p (a b)")[:, c0:c1]
                A("dve", lambda bv=bv, c0=c0, c1=c1: nc.vector.tensor_copy(out=bv, in_=ps[0][0:64, 0:c1 - c0]),
                  reads=["ps0"], writes=["Bc"])
                A("act", lambda fv=fv, c0=c0, c1=c1: nc.scalar.activation(out=fv, in_=ps[1][:, 0:c1 - c0], func=AF.Exp),
                  reads=["ps1"], writes=["FD"])
                A("dve", lambda tv=tv, liv=liv, bv=bv: nc.vector.tensor_tensor(out=tv, in0=liv, in1=bv, op=ALU.subtract),
                  reads=["LI", "Bc"], writes=["tmpg"])
                A("act", lambda wkv=wkv, tv=tv: nc.scalar.activation(out=wkv, in_=tv, func=AF.Exp),
                  reads=["tmpg"], writes=["WK"])
                A("act", lambda flv=flv, bv=bv: nc.scalar.activation(out=flv, in_=bv, func=AF.Exp, scale=-1.0),
                  reads=["Bc"], writes=["FL"])
                A("dve", lambda tv=tv, c0=c0, c1=c1: nc.vector.tensor_tensor(out=tv, in0=ps[1][0:64, 0:c1 - c0], in1=tv, op=ALU.add),
                  reads=["ps1", "tmpg"], writes=["tmpg"])
                A("act", lambda wk2v=wk2v, tv=tv: nc.scalar.activation(out=wk2v, in_=tv, func=AF.Exp),
                  reads=["tmpg"], writes=["WK2"])
        X = [sbm("X%d" % d, [128, 2, 257], F32) for d in range(2)]
        Xb = [sbm("Xb%d" % d, [128, 2, 257], BF16) for d in range(2)]
        qTs = [[sbm("mq%d_%d" % (d, i), [128, 2, BT], BF16) for i in range(2)] for d in range(2)]
        kTs = [[sbm("mk%d_%d" % (d, i), [128, 2, BT], BF16) for i in range(2)] for d in range(2)]
        ktk = [[sbm("mkt%d_%d" % (d, i), [64, BLK, 256], BF16) for i in range(2)] for d in range(2)]
        vex = [[sbm("mv%d_%d" % (d, i), [64, BLK, 257], BF16) for i in range(2)] for d in range(2)]
        Asb = [[sbm("mA%d_%d" % (d, i), [64, 64], BF16) for i in range(2)] for d in range(2)]
        vw = [[sbm("vw%d_%d" % (d, i), [64, 257], BF16) for i in range(2)] for d in range(2)]
        vw2 = [[sbm("vv%d_%d" % (d, i), [64, 257], BF16) for i in range(2)] for d in range(2)]
        dn = [[sbm("dn%d_%d" % (d, i), [64, 2], F32) for i in range(2)] for d in range(2)]
        ho = [[sbm("mho%d_%d" % (d, i), [64, 256], F32) for i in range(2)] for d in range(2)]
        for d in range(2):
            for i in range(2):
                A("pool", lambda d=d, i=i: nc.gpsimd.memset(vex[d][i][:], 1.0), writes=["mv%d_%d" % (d, i)])
        for hd in range(4):
            for d in range(2):
                A("pool", lambda d=d: nc.gpsimd.memset(X[d][:], 0.0), writes=["X%d" % d])
                A("pool", lambda d=d: nc.gpsimd.memset(Xb[d][:], 0.0), writes=["Xb%d" % d])
            for bi in range(NB):
                sl = bi % 2
                blks = (bi, NB - 1 - bi)
                for d in range(2):
                    blk = blks[d]
                    ts = slice(blk * BT, (blk + 1) * BT)
                    for dc in range(2):
                        A("sp", lambda d=d, sl=sl, dc=dc, ts=ts: nc.sync.dma_start(
                            out=qTs[d][sl][:, dc, :], in_=featT[FC_MQ + hd * 2 + dc, :, ts]),
                          writes=["mq%d_%d" % (d, sl)], dma=True)
                        A("sp", lambda d=d, sl=sl, dc=dc, ts=ts: nc.sync.dma_start(
                            out=kTs[d][sl][:, dc, :], in_=featT[FC_MK + hd * 2 + dc, :, ts]),
                          writes=["mk%d_%d" % (d, sl)], dma=True)
                    A("sp", lambda d=d, sl=sl, ts=ts: nc.sync.dma_start(
                        out=ktk[d][sl][:], in_=tokmaj[ts, hd * 256:(hd + 1) * 256].rearrange("(c j) f -> j c f", j=64)),
                      writes=["mkt%d_%d" % (d, sl)], dma=True)
                    A("sp", lambda d=d, sl=sl, ts=ts: nc.sync.dma_start(
                        out=vex[d][sl][:, :, 0:256],
                        in_=tokmaj[ts, 1024 + hd * 256:1024 + (hd + 1) * 256].rearrange("(c j) f -> j c f", j=64)),
                      writes=["mv%d_%d" % (d, sl)], dma=True)
                for ci in range(BLK):
                    for d in range(2):
                        blk = blks[d]
                        cl = ci if d == 0 else BLK - 1 - ci
                        c = blk * BLK + cl
                        ch = d * 4 + hd
                        par = ci % 2
                        pb = 4 * d
                        cs64 = slice(cl * 64, (cl + 1) * 64)
                        qk = "mq%d_%d" % (d, sl); kk = "mk%d_%d" % (d, sl)
                        ktkk = "mkt%d_%d" % (d, sl); vk = "mv%d_%d" % (d, sl)
                        Xk = "X%d" % d; Xbk = "Xb%d" % d
                        if NSEG == 2 and ((d == 0 and c == SEGC) or (d == 1 and c == SEGC - 1)):
                            A("dve", lambda d=d: nc.vector.tensor_scalar(
                                out=X[d][:].rearrange("p a b -> p (a b)"), in0=X[d][:].rearrange("p a b -> p (a b)"),
                                scalar1=flg[:, 0:1], scalar2=None, op0=ALU.mult), reads=[Xk, "flg"], writes=[Xk])
                            A("act", lambda d=d: nc.scalar.copy(out=Xb[d][:].rearrange("p a b -> p (a b)"),
                                                                in_=X[d][:].rearrange("p a b -> p (a b)")),
                              reads=[Xk], writes=[Xbk])
                        for dc in range(2):
                            A("pe", lambda d=d, sl=sl, dc=dc, cs64=cs64, pb=pb: nc.tensor.matmul(
                                ps[pb][0:64, 0:64], lhsT=kTs[d][sl][:, dc, cs64], rhs=qTs[d][sl][:, dc, cs64],
                                start=(dc == 0), stop=(dc == 1)), reads=[kk, qk], writes=["ps%d" % pb])
                        msk = MASKF if d == 0 else MASKB
                        A("dve", lambda d=d, par=par, pb=pb, msk=msk: nc.vector.tensor_tensor(
                            out=Asb[d][par][:], in0=ps[pb][0:64, 0:64], in1=cm[0:64, msk, 0:64], op=ALU.mult),
                          reads=["ps%d" % pb, "cm"], writes=["mA%d_%d" % (d, par)])
                        A("pool", lambda d=d, par=par, sl=sl, cl=cl, ch=ch, c=c: nc.gpsimd.tensor_scalar(
                            out=vw[d][par][:], in0=vex[d][sl][:, cl, :], scalar1=WK[:, ch, c:c + 1], scalar2=None,
                            op0=ALU.mult), reads=[vk, "WK"], writes=["vw%d_%d" % (d, par)])
                        A("pool", lambda d=d, par=par, sl=sl, cl=cl, ch=ch, c=c: nc.gpsimd.tensor_scalar(
                            out=vw2[d][par][:], in0=vex[d][sl][:, cl, :], scalar1=WK2[:, ch, c:c + 1], scalar2=None,
                            op0=ALU.mult), reads=[vk, "WK2"], writes=["vv%d_%d" % (d, par)])
                        A("pe", lambda d=d, par=par, pb=pb: nc.tensor.matmul(
                            ps[pb + 1][0:64, 0:257], lhsT=Asb[d][par][:], rhs=vw[d][par][:], start=True, stop=False),
                          reads=["mA%d_%d" % (d, par), "vw%d_%d" % (d, par)], writes=["ps%d" % (pb + 1)])
                        for dc in range(2):
                            A("pe", lambda d=d, sl=sl, dc=dc, cs64=cs64, pb=pb: nc.tensor.matmul(
                                ps[pb + 1][0:64, 0:257], lhsT=qTs[d][sl][:, dc, cs64], rhs=Xb[d][:, dc, :],
                                start=False, stop=(dc == 1)), reads=[qk, Xbk], writes=["ps%d" % (pb + 1)])
                        for dc in range(2):
                            A("pe", lambda d=d, sl=sl, dc=dc, cl=cl, par=par, pb=pb: nc.tensor.matmul(
                                ps[pb + 2 + dc][:, 0:257], lhsT=ktk[d][sl][:, cl, dc * 128:(dc + 1) * 128],
                                rhs=vw2[d][par][:], start=True, stop=True),
                              reads=[ktkk, "vv%d_%d" % (d, par)], writes=["ps%d" % (pb + 2 + dc)])
                        for dc in range(2):
                            A("dve", lambda d=d, dc=dc, ch=ch, c=c, pb=pb: nc.vector.scalar_tensor_tensor(
                                out=X[d][:, dc, :], in0=X[d][:, dc, :], scalar=FD[:, ch, c:c + 1],
                                in1=ps[pb + 2 + dc][:, 0:257], op0=ALU.mult, op1=ALU.add),
                              reads=[Xk, "FD", "ps%d" % (pb + 2 + dc)], writes=[Xk])
                        A("dve", lambda d=d, par=par, ch=ch, c=c, pb=pb: nc.vector.tensor_tensor(
                            out=dn[d][par][:, 1:2], in0=ps[pb + 1][0:64, 256:257], in1=FL[:, ch, c:c + 1], op=ALU.max),
                          reads=["ps%d" % (pb + 1), "FL"], writes=["dn%d_%d" % (d, par)])
                        A("dve", lambda d=d, par=par, pb=pb: nc.vector.scalar_tensor_tensor(
                            out=dn[d][par][:, 0:1], in0=ps[pb + 1][0:64, 256:257], scalar=-1.0, in1=dn[d][par][:, 1:2],
                            op0=ALU.mult, op1=ALU.max),
                          reads=["ps%d" % (pb + 1), "dn%d_%d" % (d, par)], writes=["dn%d_%d" % (d, par)])
                        A("dve", lambda d=d, par=par: nc.vector.reciprocal(out=dn[d][par][:, 1:2], in_=dn[d][par][:, 0:1]),
                          reads=["dn%d_%d" % (d, par)], writes=["dn%d_%d" % (d, par)])
                        A("act", lambda d=d, par=par, pb=pb: nc.scalar.activation(
                            out=ho[d][par][:], in_=ps[pb + 1][0:64, 0:256], func=AF.Copy, scale=dn[d][par][:, 1:2]),
                          reads=["ps%d" % (pb + 1), "dn%d_%d" % (d, par)], writes=["mho%d_%d" % (d, par)])
                        A("sp", lambda d=d, par=par, c=c: nc.sync.dma_start(
                            out=hfb[d, c * 64:(c + 1) * 64, hd * 256:(hd + 1) * 256], in_=ho[d][par][:]),
                          reads=["mho%d_%d" % (d, par)], dma=True)
                        A("act", lambda d=d: nc.scalar.copy(out=Xb[d][:].rearrange("p a b -> p (a b)"),
                                                            in_=X[d][:].rearrange("p a b -> p (a b)")),
                          reads=[Xk], writes=[Xbk])
    P.barrier()
    with contextlib.ExitStack() as en:
        sbn = lambda name, shape, dt: en.enter_context(nc.sbuf_tensor(_un(name), shape, dt))
        ha = [sbn("ha%d" % i, [128, 1024], F32) for i in range(2)]
        hb_ = [sbn("hbb%d" % i, [128, 1024], F32) for i in range(2)]
        hn = [sbn("hn%d" % i, [128, 1024], F32) for i in range(2)]
        st = [sbn("nst%d" % i, [128, 12], F32) for i in range(2)]
        junk = sbn("njunk", [128, 256], BF16)
        go = [sbn("go%d" % i, [128, 8, 128], BF16) for i in range(2)]
        gz = [sbn("gz%d" % i, [128, 8, 128], BF16) for i in range(2)]
        yo = [sbn("nyo%d" % i, [128, 8, 128], BF16) for i in range(2)]
        t1 = sbn("nt1", [128, 512], F32)
        for it in range(T // 128):
            s = it % 2
            r0 = it * 128
            A("sp", lambda s=s, r0=r0: nc.sync.dma_start(out=ha[s][:], in_=hfb[0, r0:r0 + 128, :]),
              writes=["ha%d" % s], dma=True)
            A("sp", lambda s=s, r0=r0: nc.sync.dma_start(out=hb_[s][:], in_=hfb[1, r0:r0 + 128, :]),
              writes=["hbb%d" % s], dma=True)
            A("sp", lambda s=s, r0=r0: nc.sync.dma_start(
                out=go[s][:], in_=featT[FC_MO:FC_MO + 8, :, r0:r0 + 128].rearrange("c p t -> p c t")),
              writes=["go%d" % s], dma=True)
            A("sp", lambda s=s, r0=r0: nc.sync.dma_start(
                out=gz[s][:], in_=featT[FC_MZ:FC_MZ + 8, :, r0:r0 + 128].rearrange("c p t -> p c t")),
              writes=["gz%d" % s], dma=True)
            A("pool", lambda s=s: nc.gpsimd.tensor_tensor(out=ha[s][:], in0=ha[s][:], in1=hb_[s][:], op=ALU.add),
              reads=["ha%d" % s, "hbb%d" % s], writes=["ha%d" % s])
            for h in range(4):
                A("act", lambda s=s, h=h: nc.scalar.activation(out=junk[:], in_=ha[s][:, h * 256:(h + 1) * 256],
                                                              func=AF.Square, accum_out=st[s][:, h:h + 1]),
                  reads=["ha%d" % s], writes=["njunk", "nst%d" % s])
            A("act", lambda s=s: nc.scalar.activation(out=st[s][:, 4:8], in_=st[s][:, 0:4], func=AF.Sqrt,
                                                      scale=1.0 / 256.0, bias=cm[:, 7, 0:1]),
              reads=["nst%d" % s, "cm"], writes=["nst%d" % s])
            A("dve", lambda s=s: nc.vector.reciprocal(out=st[s][:, 8:12], in_=st[s][:, 4:8]),
              reads=["nst%d" % s], writes=["nst%d" % s])
            for h in range(4):
                A("dve", lambda s=s, h=h: nc.vector.tensor_scalar(
                    out=hn[s][:, h * 256:(h + 1) * 256], in0=ha[s][:, h * 256:(h + 1) * 256],
                    scalar1=st[s][:, 8 + h:9 + h], scalar2=None, op0=ALU.mult),
                  reads=["ha%d" % s, "nst%d" % s], writes=["hn%d" % s])
            for hb2 in range(2):
                b = (it * 2 + hb2) % 8
                for j in range(4):
                    fc = hb2 * 4 + j
                    A("pe", lambda s=s, b=b, j=j, fc=fc: nc.tensor.transpose(
                        out=ps[b][:, j * 128:(j + 1) * 128], in_=hn[s][:, fc * 128:(fc + 1) * 128],
                        identity=cm[:, IDENT, :]), reads=["hn%d" % s, "cm"], writes=["ps%d" % b])
                for j in range(4):
                    fc = hb2 * 4 + j
                    A("dve", lambda s=s, b=b, j=j, fc=fc: nc.vector.scalar_tensor_tensor(
                        out=t1[:, j * 128:(j + 1) * 128], in0=ps[b][:, j * 128:(j + 1) * 128],
                        scalar=vecs[:, l, 4, fc:fc + 1], in1=go[s][:, fc, :], op0=ALU.mult, op1=ALU.mult),
                      reads=["ps%d" % b, "vecs", "go%d" % s], writes=["nt1"])
                A("pool", lambda s=s, hb2=hb2: nc.gpsimd.tensor_tensor(
                    out=yo[s][:, hb2 * 4:(hb2 + 1) * 4, :].rearrange("p a b -> p (a b)"), in0=t1[:],
                    in1=gz[s][:, hb2 * 4:(hb2 + 1) * 4, :].rearrange("p a b -> p (a b)"), op=ALU.mult),
                  reads=["nt1", "gz%d" % s], writes=["nyo%d" % s])
            A("sp", lambda s=s, r0=r0: nc.sync.dma_start(
                out=ycT[8:16, :, r0:r0 + 128].rearrange("c p t -> p c t"), in_=yo[s][:]),
              reads=["nyo%d" % s], dma=True)


def host_consts(T, NSEG, prompt_like):
    SEG = T if prompt_like else T // NSEG
    pos = np.arange(T) % SEG
    row = (pos // 64).astype(np.float32); col = (pos % 64).astype(np.float32)
    inv = (10000.0 ** (-np.arange(32, dtype=np.float32) / 32)).astype(np.float32)
    ang = np.zeros((128, T), np.float32)
    for d in range(128):
        base = row if d < 64 else col
        ang[d] = base * inv[d % 32]
    ropec = np.cos(ang).astype(np.float32); ropes = np.sin(ang).astype(np.float32)
    invcnt = np.zeros((4, T), np.float32)
    for g, w in enumerate((2, 4, 8, 16)):
        lo = np.clip(pos - w // 2, 0, SEG - 1); hi = np.clip(pos + w // 2 - 1, 0, SEG - 1)
        invcnt[g] = 1.0 / (hi - lo + 1).astype(np.float32)
    flags = np.zeros((128, 8), np.float32)
    flags[:, 0] = 1.0 if prompt_like else 0.0
    if NSEG == 2 and not prompt_like:
        flags[:, 4 + 1] = -30000.0; flags[:, 4 + 2] = -30000.0
    cmat = np.zeros((128, 8, 128), np.float32)
    cmat[:, 0, :] = np.eye(128)
    for dp in range(128):
        if (dp % 64) < 32:
            cmat[dp + 32, 1, dp] = -1.0
        else:
            cmat[dp - 32, 1, dp] = 1.0
    j = np.arange(64)[:, None]; i = np.arange(64)[None, :]
    cmat[0:64, 2, 0:64] = (j <= i); cmat[0:64, 3, 0:64] = (j >= i)
    cmat[0:64, 4, 0:64] = (j <= i); cmat[0:64, 5, 0:64] = (j >= i)
    cmat[:, 6, :] = 1.0
    cmat[:, 7, 0] = EPS; cmat[:, 7, 1] = 1.0
    return dict(ropec=ropec, ropes=ropes, invcnt=invcnt, flags=flags, cmat=cmat)


_NC_CACHE = {}


def run_units(units, weights, T, NSEG_list, L, n_cores):
    key = (T, L)
    if key not in _NC_CACHE:
        _NC_CACHE[key] = build(T, 2, L)
    nc = _NC_CACHE[key]
    in_maps = []
    for (x, p, pl) in units:
        m = dict(weights)
        m["x"] = np.ascontiguousarray(x, dtype=np.float32)
        m["p"] = np.ascontiguousarray(p, dtype=np.float32)
        m.update(host_consts(T, 2, pl))
        in_maps.append(m)
    res = run_bass_kernel_spmd(nc, in_maps, core_ids=list(range(n_cores)))
    return [r["out"] for r in res.results]


def kernel(x_prompt, x_sample, p_prompt, p_sample, norm_pre, w_in, b_gate, w_pool, pool_scale,
           mlstm_norm, q_norm, k_norm, w_out, norm_post, w_ple_proj, w_ple_gate, ple_norm):
    T = 8192
    L = 2
    f = lambda a: np.ascontiguousarray(np.asarray(a), dtype=np.float32)
    weights = dict(norm_pre=f(norm_pre), w_in=f(w_in), b_gate=f(b_gate), w_pool=f(w_pool),
                   pool_scale=f(pool_scale), mlstm_norm=f(mlstm_norm), q_norm=f(q_norm), k_norm=f(k_norm),
                   w_out=f(w_out), norm_post=f(norm_post), w_ple_proj=f(w_ple_proj),
                   w_ple_gate=f(w_ple_gate), ple_norm=f(ple_norm))
    xp = np.asarray(x_prompt); xs = np.asarray(x_sample)
    pp = np.asarray(p_prompt); psm = np.asarray(p_sample)
    units = []
    for b in range(2):
        units.append((xp[b], pp[:, b], True))
    for b in range(2):
        units.append((xs[2 * b:2 * b + 2].reshape(T, D), psm[:, 2 * b:2 * b + 2].reshape(L, T, 256), False))
    outs = run_units(units, weights, T, None, L, 4)
    y_prompt = np.stack([outs[0], outs[1]], axis=0).astype(np.float32)
    y_sample = np.concatenate([outs[2].reshape(2, 4096, D), outs[3].reshape(2, 4096, D)], axis=0).astype(np.float32)
    return (y_prompt, y_sample)
```

```python
import contextlib
import numpy as np
import concourse.bass as bass
import concourse.mybir as mybir
from concourse.bass_utils import run_bass_kernel_spmd

F32 = mybir.dt.float32
BF16 = mybir.dt.bfloat16
AF = mybir.ActivationFunctionType
ALU = mybir.AluOpType

D = 4096
KC = 32
IN_W = 12304
EPS = 1e-6
O_PU, O_PZ, O_MQ, O_MK, O_MV, O_MO, O_MZ, O_MG, O_AQ, O_AK, O_AV, O_AZ = (
    0, 1024, 2048, 3072, 4096, 5120, 6144, 7168, 7184, 9232, 9744, 10256)
FC_PZ, FC_MQ, FC_MK, FC_MO, FC_MZ, FC_AQ, FC_AK, FC_AZ, NFC = 0, 8, 16, 24, 32, 40, 56, 60, 76
TOKW = 2560


class Prog:
    ENGS = ["pe", "act", "dve", "pool", "sp"]

    def __init__(self, nc, es, needs=None, n_dma_sems=40):
        self.nc = nc
        self.dry = needs is None
        self.needs = needs
        self.needs_out = []
        self.n = 0
        self.nds = n_dma_sems
        self.eng = {"pe": nc.tensor, "act": nc.scalar, "dve": nc.vector,
                    "pool": nc.gpsimd, "sp": nc.sync}
        if not self.dry:
            self.esem = {e: es.enter_context(nc.semaphore("es_" + e)) for e in self.ENGS}
            self.dsem = [es.enter_context(nc.semaphore("ds%d" % i)) for i in range(n_dma_sems)]
        self.n_sw = 2
        self.rr_sw = 0
        self.last_w = {}
        self.rd_c = {}
        self.rd_d = {}
        self.last_eng = {}
        self.dma_prev = [None] * n_dma_sems
        self.rr = 0
        self.ev = {}
        self.seq = {e: 0 for e in self.ENGS}
        self.dcount = [0] * n_dma_sems
        self.waited_c = {e: {x: 0 for x in self.ENGS} for e in self.ENGS}
        self.waited_d = {e: [0] * n_dma_sems for e in self.ENGS}

    def _mark(self, d):
        if self.dry:
            no = self.needs_out
            for j in d:
                no[j] = 1

    def _waits(self, E, dset):
        h = self.eng[E]
        for j in sorted(dset):
            kind, who, val = self.ev[j]
            if kind == "d":
                if self.waited_d[E][who] < val:
                    h.wait_ge(self.dsem[who], val)
                    self.waited_d[E][who] = val
            else:
                if who == E and E == "pe":
                    continue
                if self.waited_c[E][who] < val:
                    h.wait_ge(self.esem[who], val)
                    self.waited_c[E][who] = val

    def add(self, eng, emit, reads=(), writes=(), dma=False):
        i = self.n
        self.n += 1
        if self.dry:
            self.needs_out.append(1 if dma else 0)
        if eng in ("act", "dve"):
            xs_ = tuple("x" + r for r in reads if r.startswith("ps"))
            if xs_:
                writes = tuple(writes) + xs_
        d = set()
        last_w = self.last_w
        for r in reads:
            j = last_w.get(r)
            if j is not None:
                d.add(j)
        for w in writes:
            j = last_w.get(w)
            if j is not None:
                d.add(j)
            rc = self.rd_c.get(w)
            if rc:
                d.update(rc.values())
            rdl = self.rd_d.get(w)
            if rdl:
                d.update(rdl)
        k = None
        if dma:
            if eng == "pool":
                k = self.nds - self.n_sw + (self.rr_sw % self.n_sw)
                self.rr_sw += 1
            else:
                k = self.rr % (self.nds - self.n_sw)
                self.rr += 1
            if self.dma_prev[k] is not None:
                d.add(self.dma_prev[k])
            self.dma_prev[k] = i
        self._mark(d)
        for r in reads:
            if dma:
                self.rd_d.setdefault(r, []).append(i)
            else:
                self.rd_c.setdefault(r, {})[eng] = i
        for w in writes:
            last_w[w] = i
            if w in self.rd_c:
                self.rd_c[w] = {}
            if w in self.rd_d:
                self.rd_d[w] = []
        if not dma:
            self.last_eng[eng] = i
        if self.dry:
            return
        self._waits(eng, d)
        ins = emit()
        if dma:
            self.dcount[k] += 16
            ins.then_inc(self.dsem[k], 16)
            self.ev[i] = ("d", k, self.dcount[k])
        elif self.needs[i]:
            self.seq[eng] += 1
            ins.then_inc(self.esem[eng], 1)
            self.ev[i] = ("c", eng, self.seq[eng])
        else:
            self.ev[i] = ("c", eng, self.seq[eng] + 1)

    def barrier(self):
        i = self.n
        self.n += 1
        d = set(self.last_eng.values())
        for k in range(self.nds):
            if self.dma_prev[k] is not None:
                d.add(self.dma_prev[k])
        if self.dry:
            self.needs_out.append(1)
        self._mark(d)
        self.last_w.clear(); self.rd_c.clear(); self.rd_d.clear()
        self.last_eng = {}
        if self.dry:
            return
        self._waits("sp", d)
        ins = self.nc.sync.nop()
        self.seq["sp"] += 1
        ins.then_inc(self.esem["sp"], 1)
        self.ev = {i: ("c", "sp", self.seq["sp"])}
        for e in ("pe", "act", "dve", "pool"):
            self.eng[e].wait_ge(self.esem["sp"], self.seq["sp"])
            self.waited_c[e]["sp"] = self.seq["sp"]
        for k in range(self.nds):
            if self.dma_prev[k] is not None:
                self.ev[self.dma_prev[k]] = ("d", k, self.dcount[k])

    def finalize(self):
        if self.dry:
            return
        for k in range(self.nds):
            if self.dcount[k] > self.waited_d["sp"][k]:
                self.nc.sync.wait_ge(self.dsem[k], self.dcount[k])
        for e in ("pe", "act", "dve", "pool"):
            if self.seq[e] > self.waited_c["sp"][e]:
                self.nc.sync.wait_ge(self.esem[e], self.seq[e])


def build(T, NSEG, L):
    needs = _build(T, NSEG, L, None)
    return _build(T, NSEG, L, needs)


class _Stop(Exception):
    pass


_UN = [0]


def _un(name):
    _UN[0] += 1
    return "%s_u%d" % (name, _UN[0])


import os as _os


def _build(T, NSEG, L, needs):
    STOP = _os.environ.get("KSTOP", "")
    SEG = T // NSEG
    NC_ = T // 64
    NT128 = T // 128
    nc = bass.Bass("TRN2", target_bir_lowering=False)
    dt_in = lambda name, shape: nc.dram_tensor(name, shape, F32, kind="ExternalInput").ap()
    x_in = dt_in("x", [T, D])
    p_in = dt_in("p", [L, T, 256])
    norm_pre = dt_in("norm_pre", [L, D]); w_in = dt_in("w_in", [L, D, IN_W])
    b_gate = dt_in("b_gate", [L, 16]); w_pool = dt_in("w_pool", [L, 4, 256, 256])
    pool_scale = dt_in("pool_scale", [L, 1024]); mlstm_norm = dt_in("mlstm_norm", [L, 1024])
    q_norm = dt_in("q_norm", [L, 128]); k_norm = dt_in("k_norm", [L, 128])
    w_out = dt_in("w_out", [L, D, D]); norm_post = dt_in("norm_post", [L, D])
    w_ple_proj = dt_in("w_ple_proj", [L, 256, D]); w_ple_gate = dt_in("w_ple_gate", [L, D, D])
    ple_norm = dt_in("ple_norm", [L, D])
    ropec = dt_in("ropec", [128, T]); ropes = dt_in("ropes", [128, T])
    invcnt = dt_in("invcnt", [4, T]); flags = dt_in("flags", [128, 8])
    cmat = dt_in("cmat", [128, 8, 128])
    out = nc.dram_tensor("out", [T, D], F32, kind="ExternalOutput").ap()

    def scr(name, shape, dt):
        return nc.dram_tensor(name, shape, dt, kind="Internal").ap()
    wcols = ([O_PU + 128 * i for i in range(8)] + [O_PZ + 128 * i for i in range(8)] +
             [O_MQ + 128 * i for i in range(8)] + [O_MK + 128 * i for i in range(8)] +
             [O_MV + 128 * i for i in range(8)] + [O_MO + 128 * i for i in range(8)] +
             [O_MZ + 128 * i for i in range(8)] + [O_AQ + 128 * i for i in range(16)] +
             [O_AK + 128 * i for i in range(4)] + [O_AV + 128 * i for i in range(4)] +
             [O_AZ + 128 * i for i in range(16)])
    wid_of = {c: i for i, c in enumerate(wcols)}
    NWC = len(wcols)
    win_bf = scr("win_bf", [L, NWC, 128, KC * 128], BF16)
    wg_bf = scr("wg_bf", [L, 128, KC * 16], BF16)
    wout_bf = scr("wout_bf", [L, 32, 128, KC * 128], BF16)
    wgate_bf = scr("wgate_bf", [L, 32, 128, KC * 128], BF16)
    wproj_bf = scr("wproj_bf", [L, 32, 128, 2 * 128], BF16)
    wpool_bf = scr("wpool_bf", [L, 4, 2, 128, 2 * 128], BF16)
    xT = scr("xT", [32, 128, T], F32)
    hT = scr("hT", [32, 128, T], BF16)
    ppT = scr("ppT", [L, 2, 128, T], BF16)
    uT = scr("uT", [8, 128, T], F32)
    featT = scr("featT", [NFC, 128, T], BF16)
    tokmaj = scr("tokmaj", [T, TOKW], BF16)
    gates_s = scr("gates_s", [T, 16], F32)
    hfb = scr("hfb", [2, T, 1024], F32)
    ycT = scr("ycT", [32, 128, T], BF16)

    es = contextlib.ExitStack()
    with es:
        P = Prog(nc, es, needs)
        A = P.add
        sb = lambda name, shape, dt: es.enter_context(nc.sbuf_tensor(_un(name), shape, dt))
        ps = [es.enter_context(nc.psum_tensor("ps%d" % i, [128, 512], F32)) for i in range(8)]
        cm = sb("cm", [128, 8, 128], F32)
        cmb = sb("cmb", [128, 8, 128], BF16)
        flg = sb("flg", [128, 8], F32)
        vecs = sb("vecs", [128, L, 5, 32], F32)
        qkn = sb("qkn", [128, L, 2], F32)
        IDENT, RMAT, TRIF, TRIB, MASKF, MASKB, ONES = range(7)
        A("sp", lambda: nc.sync.dma_start(out=cm[:], in_=cmat), writes=["cm"], dma=True)
        A("sp", lambda: nc.sync.dma_start(out=flg[:], in_=flags), writes=["flg"], dma=True)
        for l in range(L):
            for vi, src in enumerate((norm_pre, norm_post, ple_norm)):
                A("sp", lambda l=l, vi=vi, src=src: nc.sync.dma_start(
                    out=vecs[:, l, vi, :], in_=src[l].rearrange("(c p) -> p c", p=128),
                    allow_slow_non_contiguous=True), writes=["vecs"], dma=True)
            for vi, src in ((3, pool_scale), (4, mlstm_norm)):
                A("sp", lambda l=l, vi=vi, src=src: nc.sync.dma_start(
                    out=vecs[:, l, vi, 0:8], in_=src[l].rearrange("(c p) -> p c", p=128),
                    allow_slow_non_contiguous=True), writes=["vecs"], dma=True)
            for vi, src in ((0, q_norm), (1, k_norm)):
                A("sp", lambda l=l, vi=vi, src=src: nc.sync.dma_start(
                    out=qkn[:, l, vi:vi + 1], in_=src[l].rearrange("(p o) -> p o", o=1),
                    allow_slow_non_contiguous=True), writes=["qkn"], dma=True)
        A("dve", lambda: nc.vector.tensor_copy(out=cmb[:], in_=cm[:]), reads=["cm"], writes=["cmb"])

        try:
            def conv(dst, src, kcn):
                A("pool", lambda: nc.gpsimd.dma_start(
                    out=dst.rearrange("p (kc j) -> p kc j", kc=kcn),
                    in_=src.rearrange("(kc p) j -> p kc j", p=128)),
                  dma=True)
            def conv_layer(l):
                for i, c0 in enumerate(wcols):
                    conv(win_bf[l, i], w_in[l][:, c0:c0 + 128], KC)
                conv(wg_bf[l], w_in[l][:, O_MG:O_MG + 16], KC)
                for i in range(32):
                    conv(wout_bf[l, i], w_out[l][:, i * 128:(i + 1) * 128], KC)
                    conv(wgate_bf[l, i], w_ple_gate[l][:, i * 128:(i + 1) * 128], KC)
                    conv(wproj_bf[l, i], w_ple_proj[l][:, i * 128:(i + 1) * 128], 2)
                for g in range(4):
                    for dd in range(2):
                        conv(wpool_bf[l, g, dd], w_pool[l, g][:, dd * 128:(dd + 1) * 128], 2)
            conv_layer(0)

            with contextlib.ExitStack() as e1:
                sb1 = lambda name, shape, dt: e1.enter_context(nc.sbuf_tensor(_un(name), shape, dt))
                xr = [sb1("xr%d" % i, [128, D], F32) for i in range(2)]
                xs = [sb1("xs%d" % i, [128, D], F32) for i in range(2)]
                pr = [sb1("pr%d" % i, [128, L * 256], F32) for i in range(2)]
                st = [sb1("st%d" % i, [128, 4], F32) for i in range(2)]
                xo = [sb1("xo%d" % i, [128, 4, 128], F32) for i in range(4)]
                ho = [sb1("ho%d" % i, [128, 4, 128], BF16) for i in range(4)]
                junk = sb1("junk1", [128, D], BF16)
                cnt = 0
                for it in range(NT128):
                    s = it % 2
                    r0 = it * 128
                    A("sp", lambda s=s, r0=r0: nc.sync.dma_start(out=xr[s][:], in_=x_in[r0:r0 + 128, :]),
                      writes=["xr%d" % s], dma=True)
                    A("sp", lambda s=s, r0=r0: nc.sync.dma_start(
                        out=pr[s][:].rearrange("p (l f) -> p l f", l=L),
                        in_=p_in[:, r0:r0 + 128, :].rearrange("l p f -> p l f")),
                      writes=["pr%d" % s], dma=True)
                    A("act", lambda s=s: nc.scalar.activation(out=junk[:], in_=xr[s][:], func=AF.Square,
                                                              accum_out=st[s][:, 0:1]),
                      reads=["xr%d" % s], writes=["junk1", "st%d" % s])
                    A("act", lambda s=s: nc.scalar.activation(out=st[s][:, 1:2], in_=st[s][:, 0:1], func=AF.Sqrt,
                                                              scale=1.0 / D, bias=cm[:, 7, 0:1]),
                      reads=["st%d" % s, "cm"], writes=["st%d" % s])
                    A("dve", lambda s=s: nc.vector.reciprocal(out=st[s][:, 2:3], in_=st[s][:, 1:2]),
                      reads=["st%d" % s], writes=["st%d" % s])
                    A("dve", lambda s=s: nc.vector.tensor_scalar(out=xs[s][:], in0=xr[s][:], scalar1=st[s][:, 2:3],
                                                                 scalar2=None, op0=ALU.mult),
                      reads=["xr%d" % s, "st%d" % s], writes=["xs%d" % s])
                    for q in range(8):
                        b0 = (cnt * 2) % 8; b1 = (cnt * 2 + 1) % 8; o = cnt % 4; cnt += 1
                        for j in range(4):
                            dc = q * 4 + j
                            A("pe", lambda s=s, b0=b0, j=j, dc=dc: nc.tensor.transpose(
                                out=ps[b0][:, j * 128:(j + 1) * 128], in_=xr[s][:, dc * 128:(dc + 1) * 128],
                                identity=cm[:, IDENT, :]), reads=["xr%d" % s, "cm"], writes=["ps%d" % b0])
                        for j in range(4):
                            dc = q * 4 + j
                            A("pe", lambda s=s, b1=b1, j=j, dc=dc: nc.tensor.transpose(
                                out=ps[b1][:, j * 128:(j + 1) * 128], in_=xs[s][:, dc * 128:(dc + 1) * 128],
                                identity=cm[:, IDENT, :]), reads=["xs%d" % s, "cm"], writes=["ps%d" % b1])
                        A("act", lambda b0=b0, o=o: nc.scalar.copy(out=xo[o][:].rearrange("p a b -> p (a b)"), in_=ps[b0][:]),
                          reads=["ps%d" % b0], writes=["xo%d" % o])
                        for j in range(4):
                            dc = q * 4 + j
                            A("dve", lambda b1=b1, o=o, j=j, dc=dc: nc.vector.tensor_scalar(
                                out=ho[o][:, j, :], in0=ps[b1][:, j * 128:(j + 1) * 128],
                                scalar1=vecs[:, 0, 0, dc:dc + 1], scalar2=None, op0=ALU.mult),
                              reads=["ps%d" % b1, "vecs"], writes=["ho%d" % o])
                        A("sp", lambda o=o, q=q, r0=r0: nc.sync.dma_start(
                            out=xT[q * 4:(q + 1) * 4, :, r0:r0 + 128].rearrange("c p t -> p c t"), in_=xo[o][:]),
                          reads=["xo%d" % o], dma=True)
                        A("sp", lambda o=o, q=q, r0=r0: nc.sync.dma_start(
                            out=hT[q * 4:(q + 1) * 4, :, r0:r0 + 128].rearrange("c p t -> p c t"), in_=ho[o][:]),
                          reads=["ho%d" % o], dma=True)
                    bq = (cnt * 2) % 8; o = cnt % 4; cnt += 1
                    for j in range(L * 2):
                        A("pe", lambda s=s, bq=bq, j=j: nc.tensor.transpose(
                            out=ps[bq][:, j * 128:(j + 1) * 128], in_=pr[s][:, j * 128:(j + 1) * 128],
                            identity=cm[:, IDENT, :]), reads=["pr%d" % s, "cm"], writes=["ps%d" % bq])
                    A("act", lambda bq=bq, o=o: nc.scalar.copy(
                        out=ho[o][:, 0:L * 2, :].rearrange("p a b -> p (a b)"), in_=ps[bq][:, 0:L * 256]),
                      reads=["ps%d" % bq], writes=["ho%d" % o])
                    A("sp", lambda o=o, r0=r0: nc.sync.dma_start(
                        out=ppT[:, :, :, r0:r0 + 128].rearrange("l c p t -> p (l c) t"), in_=ho[o][:, 0:L * 2, :]),
                      reads=["ho%d" % o], dma=True)
            P.barrier()
            if STOP == "p1":
                raise _Stop()

            for l in range(L):
                last = (l == L - 1)
                with contextlib.ExitStack() as ea:
                    sba = lambda name, shape, dt: ea.enter_context(nc.sbuf_tensor(_un(name), shape, dt))
                    TT = min(1024, T)
                    NH = TT // 512
                    hsb = sba("hsb", [128, KC, TT], BF16)
                    wt = [sba("wt%d" % i, [128, KC, 128], BF16) for i in range(3)]
                    wgt = sba("wgt", [128, KC, 16], BF16)
                    ob = [sba("ob%d" % i, [128, 512], BF16) for i in range(4)]
                    of = [sba("of%d" % i, [128, 512], F32) for i in range(2)]
                    cs = [sba("cs%d" % i, [128, 2, 512], F32) for i in range(NH)]
                    xf = sba("xf", [128, 512], BF16); sq = sba("sq", [128, 512], BF16)
                    lnb = sba("lnb", [128, 512], F32); rinv = sba("rinv", [128, 512], F32)
                    t1 = sba("t1", [128, 512], F32); t2 = sba("t2", [128, 512], F32)
                    gtb = sba("gtb", [128, 8, 16], F32)
                    rg = sba("rg", [128, 2, 128], BF16)
                    for vi in range(2):
                        A("dve", lambda vi=vi: nc.vector.tensor_scalar(
                            out=rg[:, vi, :], in0=cm[:, RMAT, :], scalar1=qkn[:, l, vi:vi + 1], scalar2=None,
                            op0=ALU.mult), reads=["cm", "qkn"], writes=["rg"])
                    A("sp", lambda: nc.sync.dma_start(out=wgt[:].rearrange("p a b -> p (a b)"), in_=wg_bf[l]),
                      writes=["wgt"], dma=True)
                    chunks = []
                    for i in range(8): chunks.append(("u", O_PU + 128 * i, i))
                    for i in range(8): chunks.append(("q", O_MQ + 128 * i, FC_MQ + i))
                    for i in range(8): chunks.append(("k", O_MK + 128 * i, FC_MK + i))
                    for i in range(8): chunks.append(("T", O_MK + 128 * i, i * 128))
                    for i in range(8): chunks.append(("T", O_MV + 128 * i, 1024 + i * 128))
                    for i in range(4): chunks.append(("T", O_AV + 128 * i, 2048 + i * 128))
                    chunks.append(("G", None, None))
                    for i in range(16): chunks.append(("rq", O_AQ + 128 * i, FC_AQ + i))
                    for i in range(4): chunks.append(("rk", O_AK + 128 * i, FC_AK + i))
                    for i in range(8): chunks.append(("silu", O_PZ + 128 * i, FC_PZ + i))
                    for i in range(8): chunks.append(("silu", O_MZ + 128 * i, FC_MZ + i))
                    for i in range(16): chunks.append(("silu", O_AZ + 128 * i, FC_AZ + i))
                    for i in range(8): chunks.append(("sig", O_MO + 128 * i, FC_MO + i))
                    _kk = _os.environ.get("KKINDS", "")
                    if _kk:
                        chunks = [c for c in chunks if c[0] in _kk.split(",")]
                    wl = [c for c in chunks if c[0] != "G"]
                    gcount = 0
                    for tt in range(T // TT):
                        t0 = tt * TT
                        A("sp", lambda t0=t0: nc.sync.dma_start(
                            out=hsb[:], in_=hT[:, :, t0:t0 + TT].rearrange("c p t -> p c t")),
                          writes=["hsb"], dma=True)
                        for hh in range(NH):
                            A("sp", lambda hh=hh, t0=t0: nc.sync.dma_start(
                                out=cs[hh][:, 0, :], in_=ropec[:, t0 + hh * 512:t0 + (hh + 1) * 512]),
                              writes=["cs%d" % hh], dma=True)
                            A("sp", lambda hh=hh, t0=t0: nc.sync.dma_start(
                                out=cs[hh][:, 1, :], in_=ropes[:, t0 + hh * 512:t0 + (hh + 1) * 512]),
                              writes=["cs%d" % hh], dma=True)
                        wi = 0
                        def wload(k, wi):
                            slot = k % 3
                            c0 = wl[wi][1]
                            A("sp", lambda slot=slot, c0=c0: nc.sync.dma_start(
                                out=wt[slot][:].rearrange("p a b -> p (a b)"), in_=win_bf[l, wid_of[c0]]),
                              writes=["wt%d" % slot], dma=True)
                        wk_ = tt * len(wl)
                        wload(wk_, 0); wload(wk_ + 1, 1)
                        for (kind, c0, dst) in chunks:
                            if kind == "G":
                                b = gcount % 2; gcount += 1
                                for sub in range(TT // 128):
                                    for kc in range(KC):
                                        A("pe", lambda b=b, sub=sub, kc=kc: nc.tensor.matmul(
                                            ps[b][:, sub * 16:(sub + 1) * 16], lhsT=hsb[:, kc, sub * 128:(sub + 1) * 128],
                                            rhs=wgt[:, kc, :], start=(kc == 0), stop=(kc == KC - 1)),
                                          reads=["hsb", "wgt"], writes=["ps%d" % b])
                                A("dve", lambda b=b: nc.vector.tensor_copy(
                                    out=gtb[:, 0:TT // 128, :].rearrange("p a b -> p (a b)"), in_=ps[b][:, 0:(TT // 128) * 16]),
                                  reads=["ps%d" % b], writes=["gtb"])
                                A("sp", lambda t0=t0: nc.sync.dma_start(
                                    out=gates_s[t0:t0 + TT, :].rearrange("(s p) g -> p s g", p=128),
                                    in_=gtb[:, 0:TT // 128, :]), reads=["gtb"], dma=True)
                                continue
                            slot = (wk_ + wi) % 3
                            if wi + 2 < len(wl):
                                wload(wk_ + wi + 2, wi + 2)
                            wi += 1
                            wkey = "wt%d" % slot
                            if kind == "T":
                                for h4 in range(TT // 512):
                                    b = gcount % 2; gcount += 1
                                    o = gcount % 4
                                    for s4 in range(4):
                                        sub = h4 * 4 + s4
                                        for kc in range(KC):
                                            A("pe", lambda b=b, s4=s4, sub=sub, kc=kc, slot=slot: nc.tensor.matmul(
                                                ps[b][:, s4 * 128:(s4 + 1) * 128],
                                                lhsT=hsb[:, kc, sub * 128:(sub + 1) * 128], rhs=wt[slot][:, kc, :],
                                                start=(kc == 0), stop=(kc == KC - 1)),
                                              reads=["hsb", wkey], writes=["ps%d" % b])
                                    A("act", lambda b=b, o=o: nc.scalar.copy(out=ob[o][:], in_=ps[b][:]),
                                      reads=["ps%d" % b], writes=["ob%d" % o])
                                    A("sp", lambda o=o, t0=t0, h4=h4, dst=dst: nc.sync.dma_start(
                                        out=tokmaj[t0 + h4 * 512:t0 + (h4 + 1) * 512, dst:dst + 128].rearrange(
                                            "(s p) c -> p s c", p=128),
                                        in_=ob[o][:].rearrange("p (s c) -> p s c", s=4)),
                                      reads=["ob%d" % o], dma=True)
                                continue
                            for hh in range(NH):
                                b = gcount % 2; gcount += 1
                                o = gcount % 4
                                for kc in range(KC):
                                    A("pe", lambda b=b, hh=hh, kc=kc, slot=slot: nc.tensor.matmul(
                                        ps[b][:], lhsT=wt[slot][:, kc, :], rhs=hsb[:, kc, hh * 512:(hh + 1) * 512],
                                        start=(kc == 0), stop=(kc == KC - 1)),
                                      reads=["hsb", wkey], writes=["ps%d" % b])
                                tsl = slice(t0 + hh * 512, t0 + (hh + 1) * 512)
                                pk = "ps%d" % b
                                if kind == "u":
                                    of_i = gcount % 2
                                    A("act", lambda b=b, of_i=of_i: nc.scalar.copy(out=of[of_i][:], in_=ps[b][:]),
                                      reads=[pk], writes=["of%d" % of_i])
                                    A("sp", lambda of_i=of_i, dst=dst, tsl=tsl: nc.sync.dma_start(
                                        out=uT[dst, :, tsl], in_=of[of_i][:]),
                                      reads=["of%d" % of_i], dma=True)
                                    continue
                                if kind in ("q", "k"):
                                    sc = 1.0 / 16.0 if kind == "q" else 1.0
                                    A("dve", lambda b=b, o=o, sc=sc: nc.vector.tensor_scalar(
                                        out=ob[o][:], in0=ps[b][:], scalar1=sc, scalar2=None, op0=ALU.mult),
                                      reads=[pk], writes=["ob%d" % o])
                                elif kind in ("silu", "sig"):
                                    fn = AF.Silu if kind == "silu" else AF.Sigmoid
                                    A("act", lambda b=b, o=o, fn=fn: nc.scalar.activation(out=ob[o][:], in_=ps[b][:], func=fn),
                                      reads=[pk], writes=["ob%d" % o])
                                else:
                                    vi = 0 if kind == "rq" else 1
                                    A("dve", lambda b=b: nc.vector.tensor_copy(out=xf[:], in_=ps[b][:]),
                                      reads=[pk], writes=["xf"])
                                    A("act", lambda b=b: nc.scalar.activation(out=sq[:], in_=ps[b][:], func=AF.Square),
                                      reads=[pk], writes=["sq"])
                                    A("pe", lambda: nc.tensor.matmul(ps[2][:], lhsT=cmb[:, ONES, :], rhs=sq[:],
                                                                     start=True, stop=True),
                                      reads=["cmb", "sq"], writes=["ps2"])
                                    A("pe", lambda vi=vi: nc.tensor.matmul(ps[3][:], lhsT=rg[:, vi, :], rhs=xf[:],
                                                                           start=True, stop=True),
                                      reads=["rg", "xf"], writes=["ps3"])
                                    A("act", lambda: nc.scalar.activation(out=lnb[:], in_=ps[2][:], func=AF.Ln,
                                                                          scale=1.0 / 128.0, bias=cm[:, 7, 0:1]),
                                      reads=["ps2", "cm"], writes=["lnb"])
                                    A("act", lambda: nc.scalar.activation(out=rinv[:], in_=lnb[:], func=AF.Exp, scale=-0.5),
                                      reads=["lnb"], writes=["rinv"])
                                    A("dve", lambda hh=hh, vi=vi, b=b: nc.vector.scalar_tensor_tensor(
                                        out=t1[:], in0=ps[b][:], scalar=qkn[:, l, vi:vi + 1], in1=cs[hh][:, 0, :],
                                        op0=ALU.mult, op1=ALU.mult), reads=[pk, "qkn", "cs%d" % hh], writes=["t1"])
                                    A("dve", lambda hh=hh: nc.vector.tensor_tensor(
                                        out=t2[:], in0=ps[3][:], in1=cs[hh][:, 1, :], op=ALU.mult),
                                      reads=["ps3", "cs%d" % hh], writes=["t2"])
                                    A("pool", lambda: nc.gpsimd.tensor_tensor(out=t1[:], in0=t1[:], in1=t2[:], op=ALU.add),
                                      reads=["t1", "t2"], writes=["t1"])
                                    A("dve", lambda o=o: nc.vector.tensor_tensor(out=ob[o][:], in0=t1[:], in1=rinv[:],
                                                                                 op=ALU.mult),
                                      reads=["t1", "rinv"], writes=["ob%d" % o])
                                A("sp", lambda o=o, dst=dst, tsl=tsl: nc.sync.dma_start(
                                    out=featT[dst, :, tsl], in_=ob[o][:]),
                                  reads=["ob%d" % o], dma=True)
                P.barrier()
                if STOP == "pa":
                    raise _Stop()
                with contextlib.ExitStack() as eb:
                    sbb = lambda name, shape, dt: eb.enter_context(nc.sbuf_tensor(_un(name), shape, dt))
                    wp = sbb("wp", [128, 4, 2, 256], BF16)
                    A("sp", lambda: nc.sync.dma_start(out=wp[:].rearrange("p g d f -> p (g d) f"),
                                                      in_=wpool_bf[l].rearrange("g d p f -> p (g d) f")),
                      writes=["wp"], dma=True)
                    ub = [[sbb("ub%d_%d" % (i, cc), [128, 528], F32) for cc in range(2)] for i in range(2)]
                    sa = [sbb("sa%d" % cc, [128, 528], F32) for cc in range(2)]
                    sb_ = [sbb("sbb%d" % cc, [128, 528], F32) for cc in range(2)]
                    ic = [sbb("ic%d" % i, [128, 512], F32) for i in range(2)]
                    db = [[sbb("db%d_%d" % (i, cc), [128, 512], BF16) for cc in range(2)] for i in range(2)]
                    zb = [[sbb("zb%d_%d" % (i, dd), [128, 512], BF16) for dd in range(2)] for i in range(2)]
                    yo = [sbb("yo%d" % i, [128, 512], BF16) for i in range(4)]
                    tmp = sbb("ptmp", [128, 512], F32)
                    it = 0
                    for g in range(4):
                        w = (2, 4, 8, 16)[g]
                        for tl in range(T // 512):
                            t0 = tl * 512
                            s = it % 2; it += 1
                            lo = t0 - 8; hi = t0 + 520
                            segstart = (t0 % SEG == 0); segend = ((t0 + 512) % SEG == 0)
                            for cc in range(2):
                                key = "ub%d_%d" % (s, cc)
                                u = ub[s][cc]
                                fc = g * 2 + cc
                                a0 = 8 if t0 == 0 else 0
                                a1 = 520 if t0 + 512 == T else 528
                                if a0 or a1 != 528:
                                    A("pool", lambda u=u: nc.gpsimd.memset(u[:], 0.0), writes=[key])
                                A("sp", lambda u=u, fc=fc, a0=a0, a1=a1, lo=lo: nc.sync.dma_start(
                                    out=u[:, a0:a1], in_=uT[fc, :, lo + a0:lo + a1]), writes=[key], dma=True)
                                if segstart and t0 != 0:
                                    A("pool", lambda u=u: nc.gpsimd.tensor_scalar(
                                        out=u[:, 0:8], in0=u[:, 0:8], scalar1=flg[:, 0:1], scalar2=None, op0=ALU.mult),
                                      reads=[key, "flg"], writes=[key])
                                if segend and t0 + 512 != T:
                                    A("pool", lambda u=u: nc.gpsimd.tensor_scalar(
                                        out=u[:, 520:528], in0=u[:, 520:528], scalar1=flg[:, 0:1], scalar2=None,
                                        op0=ALU.mult), reads=[key, "flg"], writes=[key])
                            A("sp", lambda s=s, g=g, t0=t0: nc.sync.dma_start(
                                out=ic[s][:], in_=invcnt[g:g + 1, t0:t0 + 512].partition_broadcast(128)),
                              writes=["ic%d" % s], dma=True)
                            for dd in range(2):
                                A("sp", lambda s=s, dd=dd, g=g, t0=t0: nc.sync.dma_start(
                                    out=zb[s][dd][:], in_=featT[FC_PZ + g * 2 + dd, :, t0:t0 + 512]),
                                  writes=["zb%d_%d" % (s, dd)], dma=True)
                            for cc in range(2):
                                u = ub[s][cc]; ukey = "ub%d_%d" % (s, cc)
                                eng = "dve" if cc == 0 else "pool"
                                E = nc.vector if cc == 0 else nc.gpsimd
                                cur = u; ckey = ukey
                                steps = [(1, 0)]
                                if w >= 4: steps.append((1, -1))
                                if w >= 8: steps.append((2, -2))
                                if w >= 16: steps.append((4, -4))
                                bufs = [sa[cc], sb_[cc]]; bkeys = ["sa%d" % cc, "sbb%d" % cc]
                                margin = 0
                                for si, (sh_l, sh_r) in enumerate(steps):
                                    dst_ = bufs[si % 2]; dkey = bkeys[si % 2]
                                    if si == 0:
                                        c_lo, c_hi = 1, 528
                                        A(eng, lambda E=E, dst_=dst_, cur=cur, c_lo=c_lo, c_hi=c_hi: E.tensor_tensor(
                                            out=dst_[:, c_lo:c_hi], in0=cur[:, c_lo - 1:c_hi - 1], in1=cur[:, c_lo:c_hi],
                                            op=ALU.add), reads=[ckey], writes=[dkey])
                                        vlo, vhi = 1, 528
                                    else:
                                        k = sh_l
                                        c_lo, c_hi = vlo + k, vhi - k
                                        A(eng, lambda E=E, dst_=dst_, cur=cur, c_lo=c_lo, c_hi=c_hi, k=k: E.tensor_tensor(
                                            out=dst_[:, c_lo:c_hi], in0=cur[:, c_lo - k:c_hi - k], in1=cur[:, c_lo + k:c_hi + k],
                                            op=ALU.add), reads=[ckey], writes=[dkey])
                                        vlo, vhi = c_lo, c_hi
                                    cur = dst_; ckey = dkey
                                assert vlo <= 8 and vhi >= 520
                                A(eng, lambda E=E, cur=cur, s=s: E.tensor_tensor(
                                    out=tmp[:] if False else cur[:, 8:520], in0=cur[:, 8:520], in1=ic[s][:], op=ALU.mult),
                                  reads=[ckey, "ic%d" % s], writes=[ckey])
                                A(eng, lambda E=E, cur=cur, s=s, cc=cc, u=u: E.tensor_tensor(
                                    out=db[s][cc][:], in0=cur[:, 8:520], in1=u[:, 8:520], op=ALU.subtract),
                                  reads=[ckey, ukey], writes=["db%d_%d" % (s, cc)])
                            for dd in range(2):
                                b = 4 + (it * 2 + dd) % 4
                                o = (it * 2 + dd) % 4
                                for cc in range(2):
                                    A("pe", lambda b=b, g=g, dd=dd, cc=cc, s=s: nc.tensor.matmul(
                                        ps[b][:], lhsT=wp[:, g, dd, cc * 128:(cc + 1) * 128], rhs=db[s][cc][:],
                                        start=(cc == 0), stop=(cc == 1)),
                                      reads=["wp", "db%d_%d" % (s, cc)], writes=["ps%d" % b])
                                A("dve", lambda b=b, o=o, g=g, dd=dd, s=s: nc.vector.scalar_tensor_tensor(
                                    out=yo[o][:], in0=ps[b][:], scalar=vecs[:, l, 3, g * 2 + dd:g * 2 + dd + 1],
                                    in1=zb[s][dd][:], op0=ALU.mult, op1=ALU.mult),
                                  reads=["ps%d" % b, "vecs", "zb%d_%d" % (s, dd)], writes=["yo%d" % o])
                                A("sp", lambda o=o, g=g, dd=dd, t0=t0: nc.sync.dma_start(
                                    out=ycT[g * 2 + dd, :, t0:t0 + 512], in_=yo[o][:]),
                                  reads=["yo%d" % o], dma=True)
                P.barrier()
                if STOP == "pb1":
                    raise _Stop()
                build_mlstm(nc, P, l, T, NSEG, ps, cm, cmb, flg, vecs, featT, tokmaj, gates_s, hfb, ycT, b_gate)
                P.barrier()
                if STOP == "pb2":
                    raise _Stop()
                with contextlib.ExitStack() as ec:
                    sbc = lambda name, shape, dt: ec.enter_context(nc.sbuf_tensor(_un(name), shape, dt))
                    kT = sbc("kTs", [128, T], BF16)
                    vS = sbc("vS", [128, NT128, 128], BF16)
                    qS = [sbc("qS%d" % i, [128, 512], BF16) for i in range(2)]
                    zS = [sbc("zS%d" % i, [128, 512], BF16) for i in range(2)]
                    pT = [sbc("pT%d" % i, [128, 512], BF16) for i in range(3)]
                    rd = sbc("rd", [128, 512], F32)
                    o1 = sbc("o1", [128, 512], F32)
                    yo = [sbc("ayo%d" % i, [128, 512], BF16) for i in range(2)]
                    itq = 0
                    NKT = T // 128
                    if l + 1 < L:
                        conv_layer(l + 1)
                    iters = [(g, hq, qt) for g in range(4) for hq in range(4) for qt in range(T // 512)]

                    def load_qz(i):
                        g_, hq_, qt_ = iters[i]
                        s_ = i % 2
                        hd_ = g_ * 4 + hq_
                        q0_ = qt_ * 512
                        A("sp", lambda: nc.sync.dma_start(out=qS[s_][:], in_=featT[FC_AQ + hd_, :, q0_:q0_ + 512]),
                          writes=["qS%d" % s_], dma=True)
                        A("sp", lambda: nc.sync.dma_start(out=zS[s_][:], in_=featT[FC_AZ + hd_, :, q0_:q0_ + 512]),
                          writes=["zS%d" % s_], dma=True)
                    for it_i, (g, hq, qt) in enumerate(iters):
                        if hq == 0 and qt == 0:
                            A("sp", lambda g=g: nc.sync.dma_start(out=kT[:], in_=featT[FC_AK + g]),
                              writes=["kTs"], dma=True)
                            A("sp", lambda g=g: nc.sync.dma_start(
                                out=vS[:], in_=tokmaj[:, 2048 + g * 128:2048 + (g + 1) * 128].rearrange("(n p) c -> p n c", p=128)),
                              writes=["vS"], dma=True)
                        if it_i == 0:
                            load_qz(0)
                        if it_i + 1 < len(iters):
                            load_qz(it_i + 1)
                        if True:
                            if True:
                                hd = g * 4 + hq
                                s = it_i % 2
                                q0 = qt * 512
                                qseg = q0 // SEG
                                bo = 4 + 2 * s; bd = 5 + 2 * s

                                def smm(kt, s=s):
                                    b = kt % 2
                                    A("pe", lambda: nc.tensor.matmul(ps[b][:], lhsT=kT[:, kt * 128:(kt + 1) * 128],
                                                                     rhs=qS[s][:], start=True, stop=True),
                                      reads=["kTs", "qS%d" % s], writes=["ps%d" % b])

                                def expo(kt, qseg=qseg):
                                    b = kt % 2; pi = kt % 3
                                    kseg = (kt * 128) // SEG
                                    col = 4 + (kseg * 2 + qseg if NSEG == 2 else 0)
                                    A("act", lambda: nc.scalar.activation(
                                        out=pT[pi][:], in_=ps[b][:], func=AF.Exp, scale=128.0 ** -0.5,
                                        bias=flg[:, col:col + 1]),
                                      reads=["ps%d" % b, "flg"], writes=["pT%d" % pi])

                                def pv(kt, bo=bo, bd=bd):
                                    pi = kt % 3
                                    A("pe", lambda: nc.tensor.matmul(ps[bo][:], lhsT=vS[:, kt, :], rhs=pT[pi][:],
                                                                     start=(kt == 0), stop=(kt == NKT - 1)),
                                      reads=["vS", "pT%d" % pi], writes=["ps%d" % bo])
                                    A("pe", lambda: nc.tensor.matmul(ps[bd][:], lhsT=cmb[:, ONES, :], rhs=pT[pi][:],
                                                                     start=(kt == 0), stop=(kt == NKT - 1)),
                                      reads=["cmb", "pT%d" % pi], writes=["ps%d" % bd])
                                smm(0)
                                for kt in range(NKT):
                                    if kt + 1 < NKT:
                                        smm(kt + 1)
                                    expo(kt)
                                    pv(kt)
                                A("dve", lambda bd=bd: nc.vector.reciprocal(out=rd[:], in_=ps[bd][:]),
                                  reads=["ps%d" % bd], writes=["rd"])
                                A("dve", lambda bo=bo: nc.vector.tensor_tensor(out=o1[:], in0=ps[bo][:], in1=rd[:], op=ALU.mult),
                                  reads=["ps%d" % bo, "rd"], writes=["o1"])
                                A("dve", lambda s=s: nc.vector.tensor_tensor(out=yo[s][:], in0=o1[:], in1=zS[s][:], op=ALU.mult),
                                  reads=["o1", "zS%d" % s], writes=["ayo%d" % s])
                                A("sp", lambda s=s, hd=hd, q0=q0: nc.sync.dma_start(
                                    out=ycT[16 + hd, :, q0:q0 + 512], in_=yo[s][:]),
                                  reads=["ayo%d" % s], dma=True)
                P.barrier()
                if STOP == "pb3":
                    raise _Stop()
                with contextlib.ExitStack() as ed:
                    sbd = lambda name, shape, dt: ed.enter_context(nc.sbuf_tensor(_un(name), shape, dt))
                    ycs = sbd("ycs", [128, KC, 512], BF16)
                    yb = sbd("yb", [128, KC, 512], F32)
                    wt = [sbd("cwt%d" % i, [128, KC, 128], BF16) for i in range(3)]
                    wpj = [sbd("wpj%d" % i, [128, 2, 128], BF16) for i in range(2)]
                    pTs = sbd("pTs", [128, 2, 512], BF16)
                    sqb = [sbd("sqb%d" % i, [128, 512], BF16) for i in range(2)]
                    xc = [sbd("xc%d" % i, [128, 512], F32) for i in range(3)]
                    x1o = [sbd("x1o%d" % i, [128, 512], F32) for i in range(2)]
                    sg = [sbd("sg%d" % i, [128, 512], F32) for i in range(2)]
                    lnb = sbd("clnb", [128, 512], F32); rinv = sbd("crinv", [128, 512], F32)
                    tq = sbd("ctq", [128, 512], F32)
                    hob = [sbd("hob%d" % i, [128, 512], BF16) for i in range(2)]
                    otile = [sbd("otile%d" % i, [128, 1024], F32) for i in range(2)] if last else None
                    wcount = 0

                    def wload(src, slot):
                        A("sp", lambda: nc.sync.dma_start(out=wt[slot][:].rearrange("p a b -> p (a b)"), in_=src),
                          writes=["cwt%d" % slot], dma=True)

                    def norm_from(bank):
                        flush()
                        A("act", lambda: nc.scalar.activation(out=lnb[:], in_=ps[bank][:], func=AF.Ln,
                                                              scale=1.0 / D, bias=cm[:, 7, 0:1]),
                          reads=["ps%d" % bank, "cm"], writes=["clnb"])
                        A("act", lambda: nc.scalar.activation(out=rinv[:], in_=lnb[:], func=AF.Exp, scale=-0.5),
                          reads=["clnb"], writes=["crinv"])

                    pend = []

                    def flush():
                        while pend:
                            pend.pop(0)()

                    def sq_acc(src_ap, src_keys, idx, bank):
                        flush()
                        si = idx % 2
                        A("act", lambda: nc.scalar.activation(out=sqb[si][:], in_=src_ap, func=AF.Square),
                          reads=src_keys, writes=["sqb%d" % si])
                        pend.append(lambda: A("pe", lambda: nc.tensor.matmul(
                            ps[bank][:], lhsT=cmb[:, ONES, :], rhs=sqb[si][:], start=(idx == 0), stop=(idx == 31)),
                            reads=["cmb", "sqb%d" % si], writes=["ps%d" % bank]))
                    for tl in range(T // 512):
                        t0 = tl * 512
                        tsl = slice(t0, t0 + 512)
                        A("sp", lambda tsl=tsl: nc.sync.dma_start(out=ycs[:], in_=ycT[:, :, tsl].rearrange("c p t -> p c t")),
                          writes=["ycs"], dma=True)
                        A("sp", lambda tsl=tsl: nc.sync.dma_start(out=pTs[:], in_=ppT[l, :, :, tsl].rearrange("c p t -> p c t")),
                          writes=["pTs"], dma=True)
                        wload(wout_bf[l, 0], wcount % 3); wload(wout_bf[l, 1], (wcount + 1) % 3)
                        for oc in range(32):
                            slot = (wcount + oc) % 3
                            if oc + 2 < 32:
                                wload(wout_bf[l, oc + 2], (wcount + oc + 2) % 3)
                            b = oc % 2
                            for kc in range(KC):
                                A("pe", lambda b=b, kc=kc, slot=slot: nc.tensor.matmul(
                                    ps[b][:], lhsT=wt[slot][:, kc, :], rhs=ycs[:, kc, :], start=(kc == 0), stop=(kc == KC - 1)),
                                  reads=["cwt%d" % slot, "ycs"], writes=["ps%d" % b])
                            flush()
                            A("dve", lambda b=b, oc=oc: nc.vector.tensor_copy(out=yb[:, oc, :], in_=ps[b][:]),
                              reads=["ps%d" % b], writes=["yb%d" % oc])
                            sq_acc(ps[b][:], ["ps%d" % b], oc, 2)
                        wcount += 32
                        norm_from(2)
                        for dc in range(32):
                            xi = dc % 3; oi = dc % 2
                            A("sp", lambda xi=xi, dc=dc, tsl=tsl: nc.sync.dma_start(out=xc[xi][:], in_=xT[dc, :, tsl]),
                              reads=["xT%d" % dc], writes=["xc%d" % xi], dma=True)
                            A("dve", lambda dc=dc: nc.vector.scalar_tensor_tensor(
                                out=tq[:], in0=yb[:, dc, :], scalar=vecs[:, l, 1, dc:dc + 1], in1=rinv[:],
                                op0=ALU.mult, op1=ALU.mult), reads=["yb%d" % dc, "vecs", "crinv"], writes=["ctq"])
                            A("pool", lambda xi=xi, oi=oi: nc.gpsimd.tensor_tensor(out=x1o[oi][:], in0=tq[:], in1=xc[xi][:], op=ALU.add),
                              reads=["ctq", "xc%d" % xi], writes=["x1o%d" % oi])
                            A("act", lambda oi=oi, dc=dc: nc.scalar.copy(out=ycs[:, dc, :], in_=x1o[oi][:]),
                              reads=["x1o%d" % oi], writes=["ycs"])
                            A("sp", lambda oi=oi, dc=dc, tsl=tsl: nc.sync.dma_start(out=xT[dc, :, tsl], in_=x1o[oi][:]),
                              reads=["x1o%d" % oi], writes=["xT%d" % dc], dma=True)
                        wload(wgate_bf[l, 0], wcount % 3); wload(wgate_bf[l, 1], (wcount + 1) % 3)
                        for ecn in range(32):
                            slot = (wcount + ecn) % 3
                            if ecn + 2 < 32:
                                wload(wgate_bf[l, ecn + 2], (wcount + ecn + 2) % 3)
                            pj = ecn % 2
                            A("sp", lambda pj=pj, ecn=ecn: nc.sync.dma_start(
                                out=wpj[pj][:].rearrange("p a b -> p (a b)"), in_=wproj_bf[l, ecn]),
                              writes=["wpj%d" % pj], dma=True)
                            b = ecn % 2; b2 = 4 + ecn % 2
                            for kc in range(KC):
                                A("pe", lambda b=b, kc=kc, slot=slot: nc.tensor.matmul(
                                    ps[b][:], lhsT=wt[slot][:, kc, :], rhs=ycs[:, kc, :], start=(kc == 0), stop=(kc == KC - 1)),
                                  reads=["cwt%d" % slot, "ycs"], writes=["ps%d" % b])
                            for rc in range(2):
                                A("pe", lambda b2=b2, rc=rc, pj=pj: nc.tensor.matmul(
                                    ps[b2][:], lhsT=wpj[pj][:, rc, :], rhs=pTs[:, rc, :], start=(rc == 0), stop=(rc == 1)),
                                  reads=["wpj%d" % pj, "pTs"], writes=["ps%d" % b2])
                            flush()
                            si = ecn % 2
                            A("act", lambda b=b, si=si: nc.scalar.activation(out=sg[si][:], in_=ps[b][:], func=AF.Sigmoid),
                              reads=["ps%d" % b], writes=["sg%d" % si])
                            A("dve", lambda b2=b2, si=si, ecn=ecn: nc.vector.tensor_tensor(
                                out=yb[:, ecn, :], in0=ps[b2][:], in1=sg[si][:], op=ALU.mult),
                              reads=["ps%d" % b2, "sg%d" % si], writes=["yb%d" % ecn])
                            sq_acc(yb[:, ecn, :], ["yb%d" % ecn], ecn, 3)
                        wcount += 32
                        norm_from(3)
                        for dc in range(32):
                            xi = dc % 3
                            A("sp", lambda xi=xi, dc=dc, tsl=tsl: nc.sync.dma_start(out=xc[xi][:], in_=xT[dc, :, tsl]),
                              reads=["xT%d" % dc], writes=["xc%d" % xi], dma=True)
                            A("dve", lambda dc=dc: nc.vector.scalar_tensor_tensor(
                                out=tq[:], in0=yb[:, dc, :], scalar=vecs[:, l, 2, dc:dc + 1], in1=rinv[:],
                                op0=ALU.mult, op1=ALU.mult), reads=["yb%d" % dc, "vecs", "crinv"], writes=["ctq"])
                            A("pool", lambda xi=xi, dc=dc: nc.gpsimd.tensor_tensor(out=yb[:, dc, :], in0=tq[:], in1=xc[xi][:], op=ALU.add),
                              reads=["ctq", "xc%d" % xi], writes=["yb%d" % dc])
                            if not last:
                                A("sp", lambda dc=dc, tsl=tsl: nc.sync.dma_start(out=xT[dc, :, tsl], in_=yb[:, dc, :]),
                                  reads=["yb%d" % dc], writes=["xT%d" % dc], dma=True)
                                sq_acc(yb[:, dc, :], ["yb%d" % dc], dc, 2)
                        if not last:
                            norm_from(2)
                            for dc in range(32):
                                oi = dc % 2
                                A("dve", lambda dc=dc, oi=oi: nc.vector.scalar_tensor_tensor(
                                    out=hob[oi][:], in0=yb[:, dc, :], scalar=vecs[:, l + 1, 0, dc:dc + 1], in1=rinv[:],
                                    op0=ALU.mult, op1=ALU.mult), reads=["yb%d" % dc, "vecs", "crinv"], writes=["hob%d" % oi])
                                A("sp", lambda dc=dc, oi=oi, tsl=tsl: nc.sync.dma_start(out=hT[dc, :, tsl], in_=hob[oi][:]),
                                  reads=["hob%d" % oi], dma=True)
                        else:
                            tcnt = 0
                            for sub in range(4):
                                for q in range(4):
                                    oi = tcnt % 2; tcnt += 1
                                    for hb in range(2):
                                        b = 4 + (tcnt * 2 + hb) % 4
                                        for j in range(4):
                                            dc = q * 8 + hb * 4 + j
                                            A("pe", lambda b=b, j=j, dc=dc, sub=sub: nc.tensor.transpose(
                                                out=ps[b][:, j * 128:(j + 1) * 128], in_=yb[:, dc, sub * 128:(sub + 1) * 128],
                                                identity=cm[:, IDENT, :]), reads=["yb%d" % dc, "cm"], writes=["ps%d" % b])
                                        if hb:
                                            A("act", lambda b=b, oi=oi, hb=hb: nc.scalar.copy(
                                                out=otile[oi][:, hb * 512:(hb + 1) * 512], in_=ps[b][:]),
                                              reads=["ps%d" % b], writes=["otile%d" % oi])
                                        else:
                                            A("dve", lambda b=b, oi=oi, hb=hb: nc.vector.tensor_copy(
                                                out=otile[oi][:, hb * 512:(hb + 1) * 512], in_=ps[b][:]),
                                              reads=["ps%d" % b], writes=["otile%d" % oi])
                                    A("sp", lambda oi=oi, sub=sub, q=q, t0=t0: nc.sync.dma_start(
                                        out=out[t0 + sub * 128:t0 + (sub + 1) * 128, q * 1024:(q + 1) * 1024], in_=otile[oi][:]),
                                      reads=["otile%d" % oi], dma=True)
                P.barrier()
        except _Stop:
            P.barrier()
        P.finalize()
    if needs is None:
        return bytes(bytearray(P.needs_out))
    return nc


def build_mlstm(nc, P, l, T, NSEG, ps, cm, cmb, flg, vecs, featT, tokmaj, gates_s, hfb, ycT, b_gate):
    A = P.add
    IDENT, RMAT, TRIF, TRIB, MASKF, MASKB, ONES = range(7)
    NCK = T // 64
    BLK = min(16, NCK)
    NB = NCK // BLK
    BT = BLK * 64
    SEGC = NCK // NSEG
    with contextlib.ExitStack() as em:
        sbm = lambda name, shape, dt: em.enter_context(nc.sbuf_tensor(_un(name), shape, dt))
        Gc = sbm("Gc", [64, NCK, 16], F32)
        bg = sbm("bg", [64, 16], F32)
        pre = sbm("pre", [64, 16, NCK], F32)
        LF = sbm("LF", [64, 8, NCK], F32)
        LI = sbm("LI", [64, 8, NCK], F32)
        Bc = sbm("Bc", [64, 8, NCK], F32)
        WK = sbm("WK", [64, 8, NCK], F32)
        WK2 = sbm("WK2", [64, 8, NCK], F32)
        FL = sbm("FL", [64, 8, NCK], F32)
        FD = sbm("FD", [128, 8, NCK], F32)
        tmpg = sbm("tmpg", [64, 8, NCK], F32)
        A("sp", lambda: nc.sync.dma_start(out=Gc[:], in_=gates_s.rearrange("(c j) g -> j c g", j=64)),
          writes=["Gc"], dma=True)
        A("sp", lambda: nc.sync.dma_start(out=bg[:], in_=b_gate[l:l + 1, :].partition_broadcast(64)),
          writes=["bg"], dma=True)
        for j in range(16):
            A("dve", lambda j=j: nc.vector.tensor_scalar(out=pre[:, j, :], in0=Gc[:, :, j], scalar1=bg[:, j:j + 1],
                                                         scalar2=None, op0=ALU.add),
              reads=["Gc", "bg"], writes=["pre"])
        for d in range(2):
            fs = slice(4 + 8 * d, 8 + 8 * d); is_ = slice(8 * d, 8 * d + 4); cs_ = slice(4 * d, 4 * d + 4)
            A("act", lambda fs=fs, cs_=cs_: nc.scalar.activation(out=tmpg[:, cs_, :], in_=pre[:, fs, :], func=AF.Exp, scale=-1.0),
              reads=["pre"], writes=["tmpg"])
            A("act", lambda cs_=cs_: nc.scalar.activation(out=LF[:, cs_, :], in_=tmpg[:, cs_, :], func=AF.Ln,
                                                          bias=cm[0:64, 7, 1:2]),
              reads=["tmpg", "cm"], writes=["LF"])
            A("dve", lambda cs_=cs_: nc.vector.tensor_scalar(out=LF[:, cs_, :], in0=LF[:, cs_, :], scalar1=-1.0,
                                                             scalar2=None, op0=ALU.mult), reads=["LF"], writes=["LF"])
            A("dve", lambda is_=is_, cs_=cs_: nc.vector.tensor_copy(out=LI[:, cs_, :], in_=pre[:, is_, :]),
              reads=["pre"], writes=["LI"])
        NW = 4 * NCK
        LFh = sbm("LFh", [64, 8, NCK], BF16)
        LFl = sbm("LFl", [64, 8, NCK], BF16)
        A("dve", lambda: nc.vector.tensor_copy(out=LFh[:], in_=LF[:]), reads=["LF"], writes=["LFh"])
        A("dve", lambda: nc.vector.tensor_tensor(out=tmpg[:], in0=LF[:], in1=LFh[:], op=ALU.subtract),
          reads=["LF", "LFh"], writes=["tmpg"])
        A("dve", lambda: nc.vector.tensor_copy(out=LFl[:], in_=tmpg[:]), reads=["tmpg"], writes=["LFl"])
        for d in range(2):
            cs_ = slice(4 * d, 4 * d + 4)
            tri = TRIF if d == 0 else TRIB
            for c0 in range(0, NW, 512):
                c1 = min(NW, c0 + 512)
                lfh = LFh[:, cs_, :].rearrange("p a b -> p (a b)")[:, c0:c1]
                lfl = LFl[:, cs_, :].rearrange("p a b -> p (a b)")[:, c0:c1]
                for pi_, lfv in enumerate((lfh, lfl)):
                    A("pe", lambda tri=tri, lfv=lfv, c0=c0, c1=c1, pi_=pi_: nc.tensor.matmul(
                        ps[0][0:64, 0:c1 - c0], lhsT=cmb[0:64, tri, 0:64], rhs=lfv, start=(pi_ == 0), stop=(pi_ == 1)),
                      reads=["cmb", "LFh", "LFl"], writes=["ps0"])
                for pi_, lfv in enumerate((lfh, lfl)):
                    A("pe", lambda lfv=lfv, c0=c0, c1=c1, pi_=pi_: nc.tensor.matmul(
                        ps[1][:, 0:c1 - c0], lhsT=cmb[0:64, ONES, :], rhs=lfv, start=(pi_ == 0), stop=(pi_ == 1)),
                      reads=["cmb", "LFh", "LFl"], writes=["ps1"])
                bv = Bc[:, cs_, :].rearrange("p a b -> p (a b)")[:, c0:c1]
                fv = FD[:, cs_, :].rearrange("p a b -> p (a b)")[:, c0:c1]
                liv = LI[:, cs_, :].rearrange("p a b -> p (a b)")[:, c0:c1]
                wkv = WK[:, cs_, :].rearrange("p a b -> p (a b)")[:, c0:c1]
                wk2v = WK2[:, cs_, :].rearrange("p a b -> p (a b)")[:, c0:c1]
                flv = FL[:, cs_, :].rearrange("p a b -> p (a b)")[:, c0:c1]
                tv = tmpg[:, cs_, :].rearrange("p a b -> p (a b)")[:, c0:c1]
                A("dve", lambda bv=bv, c0=c0, c1=c1: nc.vector.tensor_copy(out=bv, in_=ps[0][0:64, 0:c1 - c0]),
                  reads=["ps0"], writes=["Bc"])
                A("act", lambda fv=fv, c0=c0, c1=c1: nc.scalar.activation(out=fv, in_=ps[1][:, 0:c1 - c0], func=AF.Exp),
                  reads=["ps1"], writes=["FD"])
                A("dve", lambda tv=tv, liv=liv, bv=bv: nc.vector.tensor_tensor(out=tv, in0=liv, in1=bv, op=ALU.subtract),
                  reads=["LI", "Bc"], writes=["tmpg"])
                A("act", lambda wkv=wkv, tv=tv: nc.scalar.activation(out=wkv, in_=tv, func=AF.Exp),
                  reads=["tmpg"], writes=["WK"])
                A("act", lambda flv=flv, bv=bv: nc.scalar.activation(out=flv, in_=bv, func=AF.Exp, scale=-1.0),
                  reads=["Bc"], writes=["FL"])
                A("dve", lambda tv=tv, c0=c0, c1=c1: nc.vector.tensor_tensor(out=tv, in0=ps[1][0:64, 0:c1 - c0], in1=tv, op=ALU.add),
                  reads=["ps1", "tmpg"], writes=["tmpg"])
                A("act", lambda wk2v=wk2v, tv=tv: nc.scalar.activation(out=wk2v, in_=tv, func=AF.Exp),
                  reads=["tmpg"], writes=["WK2"])
        X = [sbm("X%d" % d, [128, 2, 257], F32) for d in range(2)]
        Xb = [sbm("Xb%d" % d, [128, 2, 257], BF16) for d in range(2)]
        qTs = [[sbm("mq%d_%d" % (d, i), [128, 2, BT], BF16) for i in range(2)] for d in range(2)]
        kTs = [[sbm("mk%d_%d" % (d, i), [128, 2, BT], BF16) for i in range(2)] for d in range(2)]
        ktk = [[sbm("mkt%d_%d" % (d, i), [64, BLK, 256], BF16) for i in range(2)] for d in range(2)]
        vex = [[sbm("mv%d_%d" % (d, i), [64, BLK, 257], BF16) for i in range(2)] for d in range(2)]
        Asb = [[sbm("mA%d_%d" % (d, i), [64, 64], BF16) for i in range(2)] for d in range(2)]
        vw = [[sbm("vw%d_%d" % (d, i), [64, 257], BF16) for i in range(2)] for d in range(2)]
        vw2 = [[sbm("vv%d_%d" % (d, i), [64, 257], BF16) for i in range(2)] for d in range(2)]
        dn = [[sbm("dn%d_%d" % (d, i), [64, 2], F32) for i in range(2)] for d in range(2)]
        ho = [[sbm("mho%d_%d" % (d, i), [64, 256], F32) for i in range(2)] for d in range(2)]
        for d in range(2):
            for i in range(2):
                A("pool", lambda d=d, i=i: nc.gpsimd.memset(vex[d][i][:], 1.0), writes=["mv%d_%d" % (d, i)])
        for hd in range(4):
            for d in range(2):
                A("pool", lambda d=d: nc.gpsimd.memset(X[d][:], 0.0), writes=["X%d" % d])
                A("pool", lambda d=d: nc.gpsimd.memset(Xb[d][:], 0.0), writes=["Xb%d" % d])
            for bi in range(NB):
                sl = bi % 2
                blks = (bi, NB - 1 - bi)
                for d in range(2):
                    blk = blks[d]
                    ts = slice(blk * BT, (blk + 1) * BT)
                    for dc in range(2):
                        A("sp", lambda d=d, sl=sl, dc=dc, ts=ts: nc.sync.dma_start(
                            out=qTs[d][sl][:, dc, :], in_=featT[FC_MQ + hd * 2 + dc, :, ts]),
                          writes=["mq%d_%d" % (d, sl)], dma=True)
                        A("sp", lambda d=d, sl=sl, dc=dc, ts=ts: nc.sync.dma_start(
                            out=kTs[d][sl][:, dc, :], in_=featT[FC_MK + hd * 2 + dc, :, ts]),
                          writes=["mk%d_%d" % (d, sl)], dma=True)
                    A("sp", lambda d=d, sl=sl, ts=ts: nc.sync.dma_start(
                        out=ktk[d][sl][:], in_=tokmaj[ts, hd * 256:(hd + 1) * 256].rearrange("(c j) f -> j c f", j=64)),
                      writes=["mkt%d_%d" % (d, sl)], dma=True)
                    A("sp", lambda d=d, sl=sl, ts=ts: nc.sync.dma_start(
                        out=vex[d][sl][:, :, 0:256],
                        in_=tokmaj[ts, 1024 + hd * 256:1024 + (hd + 1) * 256].rearrange("(c j) f -> j c f", j=64)),
                      writes=["mv%d_%d" % (d, sl)], dma=True)
                for ci in range(BLK):
                    for d in range(2):
                        blk = blks[d]
                        cl = ci if d == 0 else BLK - 1 - ci
                        c = blk * BLK + cl
                        ch = d * 4 + hd
                        par = ci % 2
                        pb = 4 * d
                        cs64 = slice(cl * 64, (cl + 1) * 64)
                        qk = "mq%d_%d" % (d, sl); kk = "mk%d_%d" % (d, sl)
                        ktkk = "mkt%d_%d" % (d, sl); vk = "mv%d_%d" % (d, sl)
                        Xk = "X%d" % d; Xbk = "Xb%d" % d
                        if NSEG == 2 and ((d == 0 and c == SEGC) or (d == 1 and c == SEGC - 1)):
                            A("dve", lambda d=d: nc.vector.tensor_scalar(
                                out=X[d][:].rearrange("p a b -> p (a b)"), in0=X[d][:].rearrange("p a b -> p (a b)"),
                                scalar1=flg[:, 0:1], scalar2=None, op0=ALU.mult), reads=[Xk, "flg"], writes=[Xk])
                            A("act", lambda d=d: nc.scalar.copy(out=Xb[d][:].rearrange("p a b -> p (a b)"),
                                                                in_=X[d][:].rearrange("p a b -> p (a b)")),
                              reads=[Xk], writes=[Xbk])
                        for dc in range(2):
                            A("pe", lambda d=d, sl=sl, dc=dc, cs64=cs64, pb=pb: nc.tensor.matmul(
                                ps[pb][0:64, 0:64], lhsT=kTs[d][sl][:, dc, cs64], rhs=qTs[d][sl][:, dc, cs64],
                                start=(dc == 0), stop=(dc == 1)), reads=[kk, qk], writes=["ps%d" % pb])
                        msk = MASKF if d == 0 else MASKB
                        A("dve", lambda d=d, par=par, pb=pb, msk=msk: nc.vector.tensor_tensor(
                            out=Asb[d][par][:], in0=ps[pb][0:64, 0:64], in1=cm[0:64, msk, 0:64], op=ALU.mult),
                          reads=["ps%d" % pb, "cm"], writes=["mA%d_%d" % (d, par)])
                        A("pool", lambda d=d, par=par, sl=sl, cl=cl, ch=ch, c=c: nc.gpsimd.tensor_scalar(
                            out=vw[d][par][:], in0=vex[d][sl][:, cl, :], scalar1=WK[:, ch, c:c + 1], scalar2=None,
                            op0=ALU.mult), reads=[vk, "WK"], writes=["vw%d_%d" % (d, par)])
                        A("pool", lambda d=d, par=par, sl=sl, cl=cl, ch=ch, c=c: nc.gpsimd.tensor_scalar(
                            out=vw2[d][par][:], in0=vex[d][sl][:, cl, :], scalar1=WK2[:, ch, c:c + 1], scalar2=None,
                            op0=ALU.mult), reads=[vk, "WK2"], writes=["vv%d_%d" % (d, par)])
                        A("pe", lambda d=d, par=par, pb=pb: nc.tensor.matmul(
                            ps[pb + 1][0:64, 0:257], lhsT=Asb[d][par][:], rhs=vw[d][par][:], start=True, stop=False),
                          reads=["mA%d_%d" % (d, par), "vw%d_%d" % (d, par)], writes=["ps%d" % (pb + 1)])
                        for dc in range(2):
                            A("pe", lambda d=d, sl=sl, dc=dc, cs64=cs64, pb=pb: nc.tensor.matmul(
                                ps[pb + 1][0:64, 0:257], lhsT=qTs[d][sl][:, dc, cs64], rhs=Xb[d][:, dc, :],
                                start=False, stop=(dc == 1)), reads=[qk, Xbk], writes=["ps%d" % (pb + 1)])
                        for dc in range(2):
                            A("pe", lambda d=d, sl=sl, dc=dc, cl=cl, par=par, pb=pb: nc.tensor.matmul(
                                ps[pb + 2 + dc][:, 0:257], lhsT=ktk[d][sl][:, cl, dc * 128:(dc + 1) * 128],
                                rhs=vw2[d][par][:], start=True, stop=True),
                              reads=[ktkk, "vv%d_%d" % (d, par)], writes=["ps%d" % (pb + 2 + dc)])
                        for dc in range(2):
                            A("dve", lambda d=d, dc=dc, ch=ch, c=c, pb=pb: nc.vector.scalar_tensor_tensor(
                                out=X[d][:, dc, :], in0=X[d][:, dc, :], scalar=FD[:, ch, c:c + 1],
                                in1=ps[pb + 2 + dc][:, 0:257], op0=ALU.mult, op1=ALU.add),
                              reads=[Xk, "FD", "ps%d" % (pb + 2 + dc)], writes=[Xk])
                        A("dve", lambda d=d, par=par, ch=ch, c=c, pb=pb: nc.vector.tensor_tensor(
                            out=dn[d][par][:, 1:2], in0=ps[pb + 1][0:64, 256:257], in1=FL[:, ch, c:c + 1], op=ALU.max),
                          reads=["ps%d" % (pb + 1), "FL"], writes=["dn%d_%d" % (d, par)])
                        A("dve", lambda d=d, par=par, pb=pb: nc.vector.scalar_tensor_tensor(
                            out=dn[d][par][:, 0:1], in0=ps[pb + 1][0:64, 256:257], scalar=-1.0, in1=dn[d][par][:, 1:2],
                            op0=ALU.mult, op1=ALU.max),
                          reads=["ps%d" % (pb + 1), "dn%d_%d" % (d, par)], writes=["dn%d_%d" % (d, par)])
                        A("dve", lambda d=d, par=par: nc.vector.reciprocal(out=dn[d][par][:, 1:2], in_=dn[d][par][:, 0:1]),
                          reads=["dn%d_%d" % (d, par)], writes=["dn%d_%d" % (d, par)])
                        A("act", lambda d=d, par=par, pb=pb: nc.scalar.activation(
                            out=ho[d][par][:], in_=ps[pb + 1][0:64, 0:256], func=AF.Copy, scale=dn[d][par][:, 1:2]),
                          reads=["ps%d" % (pb + 1), "dn%d_%d" % (d, par)], writes=["mho%d_%d" % (d, par)])
                        A("sp", lambda d=d, par=par, c=c: nc.sync.dma_start(
                            out=hfb[d, c * 64:(c + 1) * 64, hd * 256:(hd + 1) * 256], in_=ho[d][par][:]),
                          reads=["mho%d_%d" % (d, par)], dma=True)
                        A("act", lambda d=d: nc.scalar.copy(out=Xb[d][:].rearrange("p a b -> p (a b)"),
                                                            in_=X[d][:].rearrange("p a b -> p (a b)")),
                          reads=[Xk], writes=[Xbk])
    P.barrier()
    with contextlib.ExitStack() as en:
        sbn = lambda name, shape, dt: en.enter_context(nc.sbuf_tensor(_un(name), shape, dt))
        ha = [sbn("ha%d" % i, [128, 1024], F32) for i in range(2)]
        hb_ = [sbn("hbb%d" % i, [128, 1024], F32) for i in range(2)]
        hn = [sbn("hn%d" % i, [128, 1024], F32) for i in range(2)]
        st = [sbn("nst%d" % i, [128, 12], F32) for i in range(2)]
        junk = sbn("njunk", [128, 256], BF16)
        go = [sbn("go%d" % i, [128, 8, 128], BF16) for i in range(2)]
        gz = [sbn("gz%d" % i, [128, 8, 128], BF16) for i in range(2)]
        yo = [sbn("nyo%d" % i, [128, 8, 128], BF16) for i in range(2)]
        t1 = sbn("nt1", [128, 512], F32)
        for it in range(T // 128):
            s = it % 2
            r0 = it * 128
            A("sp", lambda s=s, r0=r0: nc.sync.dma_start(out=ha[s][:], in_=hfb[0, r0:r0 + 128, :]),
              writes=["ha%d" % s], dma=True)
            A("sp", lambda s=s, r0=r0: nc.sync.dma_start(out=hb_[s][:], in_=hfb[1, r0:r0 + 128, :]),
              writes=["hbb%d" % s], dma=True)
            A("sp", lambda s=s, r0=r0: nc.sync.dma_start(
                out=go[s][:], in_=featT[FC_MO:FC_MO + 8, :, r0:r0 + 128].rearrange("c p t -> p c t")),
              writes=["go%d" % s], dma=True)
            A("sp", lambda s=s, r0=r0: nc.sync.dma_start(
                out=gz[s][:], in_=featT[FC_MZ:FC_MZ + 8, :, r0:r0 + 128].rearrange("c p t -> p c t")),
              writes=["gz%d" % s], dma=True)
            A("pool", lambda s=s: nc.gpsimd.tensor_tensor(out=ha[s][:], in0=ha[s][:], in1=hb_[s][:], op=ALU.add),
              reads=["ha%d" % s, "hbb%d" % s], writes=["ha%d" % s])
            for h in range(4):
                A("act", lambda s=s, h=h: nc.scalar.activation(out=junk[:], in_=ha[s][:, h * 256:(h + 1) * 256],
                                                              func=AF.Square, accum_out=st[s][:, h:h + 1]),
                  reads=["ha%d" % s], writes=["njunk", "nst%d" % s])
            A("act", lambda s=s: nc.scalar.activation(out=st[s][:, 4:8], in_=st[s][:, 0:4], func=AF.Sqrt,
                                                      scale=1.0 / 256.0, bias=cm[:, 7, 0:1]),
              reads=["nst%d" % s, "cm"], writes=["nst%d" % s])
            A("dve", lambda s=s: nc.vector.reciprocal(out=st[s][:, 8:12], in_=st[s][:, 4:8]),
              reads=["nst%d" % s], writes=["nst%d" % s])
            for h in range(4):
                A("dve", lambda s=s, h=h: nc.vector.tensor_scalar(
                    out=hn[s][:, h * 256:(h + 1) * 256], in0=ha[s][:, h * 256:(h + 1) * 256],
                    scalar1=st[s][:, 8 + h:9 + h], scalar2=None, op0=ALU.mult),
                  reads=["ha%d" % s, "nst%d" % s], writes=["hn%d" % s])
            for hb2 in range(2):
                b = (it * 2 + hb2) % 8
                for j in range(4):
                    fc = hb2 * 4 + j
                    A("pe", lambda s=s, b=b, j=j, fc=fc: nc.tensor.transpose(
                        out=ps[b][:, j * 128:(j + 1) * 128], in_=hn[s][:, fc * 128:(fc + 1) * 128],
                        identity=cm[:, IDENT, :]), reads=["hn%d" % s, "cm"], writes=["ps%d" % b])
                for j in range(4):
                    fc = hb2 * 4 + j
                    A("dve", lambda s=s, b=b, j=j, fc=fc: nc.vector.scalar_tensor_tensor(
                        out=t1[:, j * 128:(j + 1) * 128], in0=ps[b][:, j * 128:(j + 1) * 128],
                        scalar=vecs[:, l, 4, fc:fc + 1], in1=go[s][:, fc, :], op0=ALU.mult, op1=ALU.mult),
                      reads=["ps%d" % b, "vecs", "go%d" % s], writes=["nt1"])
                A("pool", lambda s=s, hb2=hb2: nc.gpsimd.tensor_tensor(
                    out=yo[s][:, hb2 * 4:(hb2 + 1) * 4, :].rearrange("p a b -> p (a b)"), in0=t1[:],
                    in1=gz[s][:, hb2 * 4:(hb2 + 1) * 4, :].rearrange("p a b -> p (a b)"), op=ALU.mult),
                  reads=["nt1", "gz%d" % s], writes=["nyo%d" % s])
            A("sp", lambda s=s, r0=r0: nc.sync.dma_start(
                out=ycT[8:16, :, r0:r0 + 128].rearrange("c p t -> p c t"), in_=yo[s][:]),
              reads=["nyo%d" % s], dma=True)


def host_consts(T, NSEG, prompt_like):
    SEG = T if prompt_like else T // NSEG
    pos = np.arange(T) % SEG
    row = (pos // 64).astype(np.float32); col = (pos % 64).astype(np.float32)
    inv = (10000.0 ** (-np.arange(32, dtype=np.float32) / 32)).astype(np.float32)
    ang = np.zeros((128, T), np.float32)
    for d in range(128):
        base = row if d < 64 else col
        ang[d] = base * inv[d % 32]
    ropec = np.cos(ang).astype(np.float32); ropes = np.sin(ang).astype(np.float32)
    invcnt = np.zeros((4, T), np.float32)
    for g, w in enumerate((2, 4, 8, 16)):
        lo = np.clip(pos - w // 2, 0, SEG - 1); hi = np.clip(pos + w // 2 - 1, 0, SEG - 1)
        invcnt[g] = 1.0 / (hi - lo + 1).astype(np.float32)
    flags = np.zeros((128, 8), np.float32)
    flags[:, 0] = 1.0 if prompt_like else 0.0
    if NSEG == 2 and not prompt_like:
        flags[:, 4 + 1] = -30000.0; flags[:, 4 + 2] = -30000.0
    cmat = np.zeros((128, 8, 128), np.float32)
    cmat[:, 0, :] = np.eye(128)
    for dp in range(128):
        if (dp % 64) < 32:
            cmat[dp + 32, 1, dp] = -1.0
        else:
            cmat[dp - 32, 1, dp] = 1.0
    j = np.arange(64)[:, None]; i = np.arange(64)[None, :]
    cmat[0:64, 2, 0:64] = (j <= i); cmat[0:64, 3, 0:64] = (j >= i)
    cmat[0:64, 4, 0:64] = (j <= i); cmat[0:64, 5, 0:64] = (j >= i)
    cmat[:, 6, :] = 1.0
    cmat[:, 7, 0] = EPS; cmat[:, 7, 1] = 1.0
    return dict(ropec=ropec, ropes=ropes, invcnt=invcnt, flags=flags, cmat=cmat)


_NC_CACHE = {}


def run_units(units, weights, T, NSEG_list, L, n_cores):
    key = (T, L)
    if key not in _NC_CACHE:
        _NC_CACHE[key] = build(T, 2, L)
    nc = _NC_CACHE[key]
    in_maps = []
    for (x, p, pl) in units:
        m = dict(weights)
        m["x"] = np.ascontiguousarray(x, dtype=np.float32)
        m["p"] = np.ascontiguousarray(p, dtype=np.float32)
        m.update(host_consts(T, 2, pl))
        in_maps.append(m)
    res = run_bass_kernel_spmd(nc, in_maps, core_ids=list(range(n_cores)))
    return [r["out"] for r in res.results]


def kernel(x_prompt, x_sample, p_prompt, p_sample, norm_pre, w_in, b_gate, w_pool, pool_scale,
           mlstm_norm, q_norm, k_norm, w_out, norm_post, w_ple_proj, w_ple_gate, ple_norm):
    T = 8192
    L = 2
    f = lambda a: np.ascontiguousarray(np.asarray(a), dtype=np.float32)
    weights = dict(norm_pre=f(norm_pre), w_in=f(w_in), b_gate=f(b_gate), w_pool=f(w_pool),
                   pool_scale=f(pool_scale), mlstm_norm=f(mlstm_norm), q_norm=f(q_norm), k_norm=f(k_norm),
                   w_out=f(w_out), norm_post=f(norm_post), w_ple_proj=f(w_ple_proj),
                   w_ple_gate=f(w_ple_gate), ple_norm=f(ple_norm))
    xp = np.asarray(x_prompt); xs = np.asarray(x_sample)
    pp = np.asarray(p_prompt); psm = np.asarray(p_sample)
    units = []
    for b in range(2):
        units.append((xp[b], pp[:, b], True))
    for b in range(2):
        units.append((xs[2 * b:2 * b + 2].reshape(T, D), psm[:, 2 * b:2 * b + 2].reshape(L, T, 256), False))
    outs = run_units(units, weights, T, None, L, 4)
    y_prompt = np.stack([outs[0], outs[1]], axis=0).astype(np.float32)
    y_sample = np.concatenate([outs[2].reshape(2, 4096, D), outs[3].reshape(2, 4096, D)], axis=0).astype(np.float32)
    return (y_prompt, y_sample)
```
